# Optimizing a Trainium2 kernel written in Bass

```python
import math
import jax, jax.numpy as jnp
from jax import lax
import numpy as np


D_MODEL = 1024
BATCH = 16
SEQ = 4096
DEPTH = 2

N_MIXERS = 2
N_A = (DEPTH + 1) // 2
N_B = DEPTH // 2

SSM_EXPAND = 2
D_INNER = SSM_EXPAND * D_MODEL
SSM_HEAD_DIM = 64
SSM_HEADS = D_INNER // SSM_HEAD_DIM
SSM_GROUPS = 8
SSM_HPG = SSM_HEADS // SSM_GROUPS
SSM_STATE = 128
CONV_K = 4
CONV_DIM = D_INNER + 2 * SSM_GROUPS * SSM_STATE
SSM_IN_DIM = D_INNER + CONV_DIM + SSM_HEADS
NORM_GROUP = D_INNER // SSM_GROUPS
CHUNK = 128
DT_MIN = 0.001
DT_MAX = 0.1

MLA_HEADS = 16
Q_LORA = 384
KV_LORA = 256
QK_NOPE = 64
QK_ROPE = 32
V_DIM = 64
QK_DIM = QK_NOPE + QK_ROPE
MLA_IN_DIM = Q_LORA + KV_LORA + QK_ROPE
ROPE_THETA = 10000.0
Q_BLOCK = 128

D_FF = 2816
EPS = 1e-6

kernel_name = "hybrid_ssd_mla_macaron_trunk"


def rms_norm(x, w):
    xf = x.astype(jnp.float32)
    y = xf * lax.rsqrt(jnp.mean(xf * xf, axis=-1, keepdims=True) + EPS)
    return (y * w.astype(jnp.float32)).astype(x.dtype)


def swiglu(h, w_gate, w_up, w_down):
    return (jax.nn.silu(h @ w_gate) * (h @ w_up)) @ w_down


def causal_depthwise_conv(u, w, b):
    y = lax.conv_general_dilated(u, w[:, None, :].astype(u.dtype), window_strides=(1,),
                                 padding=[(CONV_K - 1, 0)],
                                 dimension_numbers=("NWC", "WIO", "NWC"),
                                 feature_group_count=u.shape[-1])
    return y + b


def ssd_chunked(xs, dt, a, bm, cm):
    b, s = xs.shape[:2]
    nc = s // CHUNK

    def to_chunks(t):
        return jnp.swapaxes(t.reshape((b, nc, CHUNK) + t.shape[2:]), 0, 1)

    mask = jnp.tril(jnp.ones((CHUNK, CHUNK), dtype=bool))[None, :, :, None, None]

    def step(state, inp):
        xc, dtc, bc, cc = inp
        acs = jnp.cumsum(dtc * a, axis=1)
        seg = acs[:, :, None] - acs[:, None, :]
        decay = jnp.exp(jnp.where(mask, seg, -jnp.inf))
        cb = jnp.einsum("blgn,bsgn->blsg", cc, bc)
        xdt = xc * dtc[..., None]
        y_diag = jnp.einsum("blsg,blsgr,bsgrp->blgrp", cb, decay, xdt)
        y_off = jnp.einsum("blgn,bgrpn->blgrp", cc, state) * jnp.exp(acs)[..., None]
        decay_to_end = jnp.exp(acs[:, -1:] - acs)
        new_state = (state * jnp.exp(acs[:, -1])[..., None, None]
                     + jnp.einsum("bsgn,bsgr,bsgrp->bgrpn", bc, decay_to_end, xdt))
        return new_state, y_diag + y_off

    state0 = jnp.zeros((b, SSM_GROUPS, SSM_HPG, SSM_HEAD_DIM, SSM_STATE), jnp.float32)
    _, ys = lax.scan(step, state0, (to_chunks(xs), to_chunks(dt), to_chunks(bm), to_chunks(cm)))
    return jnp.swapaxes(ys, 0, 1).reshape(xs.shape)


def mamba2_mixer(h, w_in, conv_w, conv_b, dt_bias, a_log, d_skip, norm_w, w_out):
    b, s, _ = h.shape
    proj = h @ w_in
    z = proj[..., :D_INNER]
    xbc = proj[..., D_INNER:D_INNER + CONV_DIM]
    dt = proj[..., D_INNER + CONV_DIM:]
    xbc = jax.nn.silu(causal_depthwise_conv(xbc, conv_w, conv_b))
    xs = xbc[..., :D_INNER].astype(jnp.float32).reshape(b, s, SSM_GROUPS, SSM_HPG, SSM_HEAD_DIM)
    bm = xbc[..., D_INNER:D_INNER + SSM_GROUPS * SSM_STATE].astype(jnp.float32).reshape(b, s, SSM_GROUPS, SSM_STATE)
    cm = xbc[..., D_INNER + SSM_GROUPS * SSM_STATE:].astype(jnp.float32).reshape(b, s, SSM_GROUPS, SSM_STATE)
    dt = jax.nn.softplus(dt.astype(jnp.float32) + dt_bias.astype(jnp.float32)).reshape(b, s, SSM_GROUPS, SSM_HPG)
    a = (-jnp.exp(a_log.astype(jnp.float32))).reshape(SSM_GROUPS, SSM_HPG)
    y = ssd_chunked(xs, dt, a, bm, cm)
    y = y + d_skip.astype(jnp.float32).reshape(SSM_GROUPS, SSM_HPG)[:, :, None] * xs
    g = (y.reshape(b, s, D_INNER) * jax.nn.silu(z.astype(jnp.float32))).reshape(b, s, SSM_GROUPS, NORM_GROUP)
    g = g * lax.rsqrt(jnp.mean(g * g, axis=-1, keepdims=True) + EPS)
    g = g.reshape(b, s, D_INNER) * norm_w.astype(jnp.float32)
    return g.astype(h.dtype) @ w_out


def rope_cos_sin(positions):
    inv_freq = 1.0 / (ROPE_THETA ** (jnp.arange(0, QK_ROPE, 2, dtype=jnp.float32) / QK_ROPE))
    ang = positions.astype(jnp.float32)[..., None] * inv_freq
    return jnp.cos(ang), jnp.sin(ang)


def apply_rope(x, cos, sin):
    xf = x.astype(jnp.float32)
    x1, x2 = xf[..., :QK_ROPE // 2], xf[..., QK_ROPE // 2:]
    return jnp.concatenate([x1 * cos - x2 * sin, x2 * cos + x1 * sin], axis=-1).astype(x.dtype)


def causal_block_attention(q_nope, q_rope, k_nope, k_rope, v):
    s = q_nope.shape[1]
    scale = QK_DIM ** -0.5
    outs = []
    for i in range(s // Q_BLOCK):
        q0 = i * Q_BLOCK
        kend = q0 + Q_BLOCK
        sc = (jnp.einsum("bqhd,bkhd->bhqk", q_nope[:, q0:kend], k_nope[:, :kend])
              + jnp.einsum("bqhr,bkr->bhqk", q_rope[:, q0:kend], k_rope[:, :kend]))
        sc = sc.astype(jnp.float32) * scale
        qi = q0 + jnp.arange(Q_BLOCK)
        ki = jnp.arange(kend)
        sc = jnp.where(ki[None, :] <= qi[:, None], sc, -jnp.inf)
        p = jax.nn.softmax(sc, axis=-1).astype(v.dtype)
        outs.append(jnp.einsum("bhqk,bkhd->bqhd", p, v[:, :kend]))
    return jnp.concatenate(outs, axis=1)


def mla_mixer(h, positions, w_in, q_a_norm, kv_a_norm, w_q_b, w_kv_b, q_norm, k_norm, w_out):
    b, s, _ = h.shape
    proj = h @ w_in
    cq = rms_norm(proj[..., :Q_LORA], q_a_norm)
    ckv = rms_norm(proj[..., Q_LORA:Q_LORA + KV_LORA], kv_a_norm)
    k_rope = proj[..., Q_LORA + KV_LORA:]
    q = (cq @ w_q_b).reshape(b, s, MLA_HEADS, QK_DIM)
    kv = (ckv @ w_kv_b).reshape(b, s, MLA_HEADS, QK_NOPE + V_DIM)
    q_nope, q_rope = q[..., :QK_NOPE], q[..., QK_NOPE:]
    k_nope, v = kv[..., :QK_NOPE], kv[..., QK_NOPE:]
    q_nope = rms_norm(q_nope, q_norm[:QK_NOPE])
    q_rope = rms_norm(q_rope, q_norm[QK_NOPE:])
    k_nope = rms_norm(k_nope, k_norm[:QK_NOPE])
    k_rope = rms_norm(k_rope, k_norm[QK_NOPE:])
    cos, sin = rope_cos_sin(positions)
    q_rope = apply_rope(q_rope, cos[:, :, None], sin[:, :, None])
    k_rope = apply_rope(k_rope, cos, sin)
    o = causal_block_attention(q_nope, q_rope, k_nope, k_rope, v)
    return o.reshape(b, s, MLA_HEADS * V_DIM) @ w_out


def setup_inputs(seed: int = 0) -> dict:
    key = jax.random.key(seed)
    ks = jax.random.split(key, 24)
    f32 = jnp.float32

    def nrm(k, shape, fan_in):
        return jax.random.normal(k, shape, f32) * fan_in ** -0.5

    def gain(k, shape):
        return 1.0 + 0.05 * jax.random.normal(k, shape, f32)

    x = jax.random.normal(ks[0], (BATCH, SEQ, D_MODEL), f32)
    offs = jax.random.randint(ks[1], (BATCH, 1), 0, 512, dtype=jnp.int32)
    positions = (jnp.arange(SEQ, dtype=jnp.int32)[None, :] + offs).astype(jnp.int32)

    norm_w = gain(ks[2], (DEPTH, 3, D_MODEL))
    ffn_w_gate = nrm(ks[3], (DEPTH, 2, D_MODEL, D_FF), D_MODEL)
    ffn_w_up = nrm(ks[4], (DEPTH, 2, D_MODEL, D_FF), D_MODEL)
    ffn_w_down = nrm(ks[5], (DEPTH, 2, D_FF, D_MODEL), D_FF)

    ssm_w_in = nrm(ks[6], (N_A, D_MODEL, SSM_IN_DIM), D_MODEL)
    ssm_conv_w = nrm(ks[7], (N_A, CONV_K, CONV_DIM), CONV_K)
    ssm_conv_b = 0.02 * jax.random.normal(ks[8], (N_A, CONV_DIM), f32)
    u = jax.random.uniform(ks[9], (N_A, SSM_HEADS), f32)
    dt0 = jnp.exp(u * (math.log(DT_MAX) - math.log(DT_MIN)) + math.log(DT_MIN))
    ssm_dt_bias = dt0 + jnp.log(-jnp.expm1(-dt0))
    ssm_a_log = jnp.log(jax.random.uniform(ks[10], (N_A, SSM_HEADS), f32, minval=1.0, maxval=16.0))
    ssm_d = 1.0 + 0.1 * jax.random.normal(ks[11], (N_A, SSM_HEADS), f32)
    ssm_norm_w = gain(ks[12], (N_A, D_INNER))
    ssm_w_out = nrm(ks[13], (N_A, D_INNER, D_MODEL), D_INNER)

    mla_w_in = nrm(ks[14], (N_B, D_MODEL, MLA_IN_DIM), D_MODEL)
    mla_q_a_norm = gain(ks[15], (N_B, Q_LORA))
    mla_kv_a_norm = gain(ks[16], (N_B, KV_LORA))
    mla_w_q_b = nrm(ks[17], (N_B, Q_LORA, MLA_HEADS * QK_DIM), Q_LORA)
    mla_w_kv_b = nrm(ks[18], (N_B, KV_LORA, MLA_HEADS * (QK_NOPE + V_DIM)), KV_LORA)
    mla_q_norm = gain(ks[19], (N_B, QK_DIM))
    mla_k_norm = gain(ks[20], (N_B, QK_DIM))
    mla_w_out = nrm(ks[21], (N_B, MLA_HEADS * V_DIM, D_MODEL), MLA_HEADS * V_DIM)

    return {"x": x, "positions": positions, "norm_w": norm_w,
            "ffn_w_gate": ffn_w_gate, "ffn_w_up": ffn_w_up, "ffn_w_down": ffn_w_down,
            "ssm_w_in": ssm_w_in, "ssm_conv_w": ssm_conv_w, "ssm_conv_b": ssm_conv_b,
            "ssm_dt_bias": ssm_dt_bias, "ssm_a_log": ssm_a_log, "ssm_d": ssm_d,
            "ssm_norm_w": ssm_norm_w, "ssm_w_out": ssm_w_out,
            "mla_w_in": mla_w_in, "mla_q_a_norm": mla_q_a_norm, "mla_kv_a_norm": mla_kv_a_norm,
            "mla_w_q_b": mla_w_q_b, "mla_w_kv_b": mla_w_kv_b, "mla_q_norm": mla_q_norm,
            "mla_k_norm": mla_k_norm, "mla_w_out": mla_w_out}


def reference(x, positions, norm_w, ffn_w_gate, ffn_w_up, ffn_w_down,
              ssm_w_in, ssm_conv_w, ssm_conv_b, ssm_dt_bias, ssm_a_log, ssm_d,
              ssm_norm_w, ssm_w_out,
              mla_w_in, mla_q_a_norm, mla_kv_a_norm, mla_w_q_b, mla_w_kv_b,
              mla_q_norm, mla_k_norm, mla_w_out):
    for i in range(DEPTH):
        x = x + 0.5 * swiglu(rms_norm(x, norm_w[i, 0]), ffn_w_gate[i, 0], ffn_w_up[i, 0], ffn_w_down[i, 0])
        h = rms_norm(x, norm_w[i, 1])
        j = i // N_MIXERS
        if i % N_MIXERS == 0:
            x = x + mamba2_mixer(h, ssm_w_in[j], ssm_conv_w[j], ssm_conv_b[j], ssm_dt_bias[j],
                                 ssm_a_log[j], ssm_d[j], ssm_norm_w[j], ssm_w_out[j])
        else:
            x = x + mla_mixer(h, positions, mla_w_in[j], mla_q_a_norm[j], mla_kv_a_norm[j],
                              mla_w_q_b[j], mla_w_kv_b[j], mla_q_norm[j], mla_k_norm[j], mla_w_out[j])
        x = x + 0.5 * swiglu(rms_norm(x, norm_w[i, 2]), ffn_w_gate[i, 1], ffn_w_up[i, 1], ffn_w_down[i, 1])
    return x
```

```python
import numpy as np
from concourse.bass_utils import run_bass_kernel_spmd
import bisect
from contextlib import ExitStack
import concourse.bass as bass
import concourse.mybir as mybir

F32 = mybir.dt.float32
BF16 = mybir.dt.bfloat16
I32 = mybir.dt.int32
AF = mybir.ActivationFunctionType
ALU = mybir.AluOpType
AX = mybir.AxisListType
ESZ = {F32: 4, BF16: 2, I32: 4}

COMPUTE = ("pe", "act", "dve", "pool")
ENGS = ("pe", "act", "dve", "pool", "sp")
DMA_POOL = 24


class V:
    def __init__(self, ap, space, base, fshape, dtype):
        self.ap = ap
        self.space = space
        self.base = base
        self.fshape = tuple(fshape)
        self.dtype = dtype
        es = ESZ[dtype]
        st = []
        s = es
        for d in reversed(self.fshape):
            st.append(s)
            s *= d
        self.fstrides = tuple(reversed(st))
        self.lo = base
        self.hi = base + s
        self.full_iv = None

    def __getitem__(self, key):
        if not isinstance(key, tuple):
            key = (key,)
        key = key + (slice(None),) * (1 + len(self.fshape) - len(key))
        ap = self.ap[key]
        lo = self.base
        hi = self.base
        newshape = []
        for k, d, st in zip(key[1:], self.fshape, self.fstrides):
            if isinstance(k, int):
                a, b = k, k + 1
            else:
                a = 0 if k.start is None else k.start
                b = d if k.stop is None else k.stop
                assert k.step in (None, 1)
                newshape.append(b - a)
            assert 0 <= a < b <= d, (key, self.fshape)
            lo += a * st
            hi += (b - 1) * st
        v = V.__new__(V)
        v.ap = ap
        v.space = self.space
        v.dtype = self.dtype
        v.base = lo
        v.fshape = tuple(newshape)
        v.fstrides = tuple(st for k, st in zip(key[1:], self.fstrides) if not isinstance(k, int))
        v.lo = lo
        v.hi = hi + ESZ[self.dtype]
        v.full_iv = self.full_iv
        if self.full_iv is not None:
            v.lo, v.hi = self.full_iv
        return v


class Key:
    def __init__(self, space, lo=0, hi=1):
        self.space, self.lo, self.hi = space, lo, hi


class Op:
    __slots__ = ("eng", "idx", "fn", "deps", "sig", "dma_sem", "dma_target", "ndma", "has_dep", "name", "label", "gidx", "cost", "nbytes", "fin", "nleft", "users", "ridy")


class MK:
    def __init__(self, nc):
        self.nc = nc
        self.streams = {e: [] for e in ENGS}
        self.segs = {}
        self.dma_count = {e: 0 for e in ENGS}
        self.dma_ops = {e: [] for e in ENGS}
        self.nsem_dma = {}
        self.label = None
        self.use_scopes = False
        self.gcount = 0

    def _segments(self, space, lo, hi):
        L = self.segs.setdefault(space, [])
        i = bisect.bisect_right(L, lo, key=lambda s: s[0]) - 1
        if i >= 0 and L[i][1] > lo:
            s = L[i]
            if s[0] < lo:
                L.insert(i + 1, [lo, s[1], s[2], list(s[3]), list(s[4])])
                s[1] = lo
                i += 1
        else:
            i += 1
        out = []
        cur = lo
        j = i
        while cur < hi:
            if j < len(L) and L[j][0] == cur:
                s = L[j]
                if s[1] > hi:
                    L.insert(j + 1, [hi, s[1], s[2], list(s[3]), list(s[4])])
                    s[1] = hi
                out.append(s)
                cur = s[1]
                j += 1
            else:
                nxt = min(hi, L[j][0]) if j < len(L) else hi
                g = [cur, nxt, None, [], []]
                L.insert(j, g)
                out.append(g)
                cur = nxt
                j += 1
        return out

    def _merge(self, space):
        L = self.segs[space]
        if len(L) < 256:
            return
        M = [L[0]]
        for s in L[1:]:
            p = M[-1]
            if p[1] == s[0] and p[2] is s[2] and p[3] == s[3] and p[4] == s[4]:
                p[1] = s[1]
            else:
                M.append(s)
        self.segs[space] = M

    def op(self, eng, fn, reads=(), writes=(), ndma=0, name=None, cost=None, nbytes=0):
        o = Op()
        o.gidx = self.gcount
        self.gcount += 1
        if cost is None:
            n = 64
            if writes and hasattr(writes[0], "fshape"):
                n = 1
                for d_ in writes[0].fshape:
                    n *= d_
            if eng == "pe":
                cost = 250.0
            elif eng == "act":
                cost = 220.0 + n / 1.2
            elif eng == "dve":
                cost = 120.0 + n / 0.96
            else:
                cost = 250.0 + n / 0.7
        o.cost = cost
        o.nbytes = nbytes
        o.eng = eng
        o.fn = fn
        o.idx = len(self.streams[eng])
        o.ndma = ndma
        o.sig = None
        o.has_dep = False
        o.name = name
        o.dma_sem = None
        o.label = self.label
        deps = {}
        is_dma = ndma > 0

        def add(d):
            if d is not None and d is not o:
                deps[id(d)] = d

        for r in reads:
            for s in self._segments(r.space, r.lo, r.hi):
                add(s[2])
                if is_dma:
                    s[4].append(o)
                else:
                    s[3].append(o)
        for w in writes:
            for s in self._segments(w.space, w.lo, w.hi):
                add(s[2])
                for rd in s[3]:
                    add(rd)
                for rd in s[4]:
                    add(rd)
                s[2] = o
                s[3] = []
                s[4] = []
            self._merge(w.space)
        if is_dma:
            k = self.dma_count[eng]
            self.dma_count[eng] += 1
            o.dma_sem = k % DMA_POOL
            lst = self.dma_ops[eng]
            if k >= DMA_POOL:
                add(lst[k - DMA_POOL])
            lst.append(o)
        o.deps = list(deps.values())
        for d in o.deps:
            d.has_dep = True
        self.streams[eng].append(o)
        return o

    def schedule(self, window=64, lat=120.0):
        allops = []
        for e in ENGS:
            allops.extend(self.streams[e])
        for o in allops:
            o.users = []
            o.fin = None
        for o in allops:
            o.nleft = len(o.deps)
            o.ridy = 0.0
            for d in o.deps:
                d.users.append(o)
        orig = {e: list(self.streams[e]) for e in ENGS}
        head = {e: 0 for e in ENGS}
        done = set()
        ready = {e: [] for e in ENGS}
        inwin = {e: 0 for e in ENGS}
        waiting = {}
        free_at = {e: 0.0 for e in ENGS}
        new = {e: [] for e in ENGS}
        dma_bw_free = [0.0]
        total = len(allops)
        nsched = 0

        def admit(e):
            lim = min(len(orig[e]), head[e] + window)
            while inwin[e] < lim:
                o = orig[e][inwin[e]]
                inwin[e] += 1
                if id(o) not in done and o.nleft == 0:
                    ready[e].append(o)
        for e in ENGS:
            admit(e)
        while nsched < total:
            best = None
            for e in ENGS:
                fa = free_at[e]
                for o in ready[e]:
                    st = o.ridy if o.ridy > fa else fa
                    k = (st, o.gidx)
                    if best is None or k < best[0]:
                        best = (k, o)
            assert best is not None, "scheduler stuck"
            (st, _), o = best
            e = o.eng
            ready[e].remove(o)
            if o.ndma > 0:
                issue = 1200.0 if e == "pool" else 120.0
                t0 = max(st + issue, dma_bw_free[0])
                dur = o.nbytes / 200.0
                dma_bw_free[0] = t0 + dur
                o.fin = t0 + dur + 1800.0
                free_at[e] = st + issue
            else:
                o.fin = st + o.cost
                free_at[e] = o.fin
            done.add(id(o))
            new[e].append(o)
            nsched += 1
            lst = orig[e]
            h = head[e]
            while h < len(lst) and id(lst[h]) in done:
                h += 1
            head[e] = h
            admit(e)
            for u in o.users:
                u.nleft -= 1
                f = o.fin + lat
                if f > u.ridy:
                    u.ridy = f
                if u.nleft == 0:
                    ue = u.eng
                    if u.idx < inwin[ue]:
                        ready[ue].append(u)
        for e in ENGS:
            self.streams[e] = new[e]
        self.makespan = max(free_at.values())
        for e in ENGS:
            k = 0
            lst = []
            for o in self.streams[e]:
                if o.ndma > 0:
                    o.dma_sem = k % DMA_POOL
                    if k >= DMA_POOL:
                        p = lst[k - DMA_POOL]
                        if p not in o.deps:
                            o.deps.append(p)
                            p.has_dep = True
                    lst.append(o)
                    k += 1
            self.dma_ops[e] = lst

    def emit(self, final_wait_ops=()):
        nc = self.nc
        with ExitStack() as es:
            CH = 16000
            nsigs = {e: sum(1 for o in self.streams[e] if o.ndma == 0 and o.has_dep) for e in COMPUTE}
            csem = {e: [es.enter_context(nc.semaphore(f"c_{e}_{i}")) for i in range(nsigs[e] // CH + 1)] for e in COMPUTE}
            dsem = {}
            for e in ENGS:
                if self.dma_count[e] > 0:
                    for i in range(min(DMA_POOL, self.dma_count[e])):
                        dsem[(e, i)] = es.enter_context(nc.semaphore(f"d_{e}_{i}"))
            for e in COMPUTE:
                c = 0
                for o in self.streams[e]:
                    if o.ndma == 0 and o.has_dep:
                        c += 1
                        o.sig = c
            for e in ENGS:
                tot = {}
                for o in self.dma_ops[e]:
                    t = tot.get(o.dma_sem, 0) + 16 * o.ndma
                    tot[o.dma_sem] = t
                    o.dma_target = t
            self.nsig = {e: sum(1 for o in self.streams[e] if o.sig) for e in COMPUTE}
            block = es.enter_context(nc.Block())
            mk = self

            def run(eng_name, E):
                known = {}
                nwait = 0
                cur = None
                cm = None
                for o in mk.streams[eng_name]:
                    if mk.use_scopes and o.label != cur:
                        if cm is not None:
                            cm.__exit__(None, None, None)
                            cm = None
                        cur = o.label
                        if cur is not None:
                            cm = nc.named_scope(cur)
                            cm.__enter__()
                    need = {}
                    for d in o.deps:
                        if d.ndma > 0:
                            key = ("d", d.eng, d.dma_sem)
                            val = d.dma_target
                        else:
                            if d.eng == "pe" and eng_name == "pe":
                                continue
                            key = ("c", d.eng)
                            val = d.sig
                        if known.get(key, 0) >= val:
                            continue
                        if need.get(key, 0) < val:
                            need[key] = val
                    for key, val in need.items():
                        if key[0] == "c":
                            E.wait_ge(csem[key[1]][(val - 1) // CH], (val - 1) % CH + 1)
                        else:
                            E.wait_ge(dsem[(key[1], key[2])], val)
                        known[key] = val
                        nwait += 1
                    r = o.fn(E)
                    if o.ndma > 0:
                        ins = r if isinstance(r, (list, tuple)) else [r]
                        assert len(ins) == o.ndma, (len(ins), o.ndma, o.name)
                        for i_ in ins:
                            i_.then_inc(dsem[(eng_name, o.dma_sem)], 16)
                    elif o.sig is not None:
                        r.then_inc(csem[eng_name][(o.sig - 1) // CH], 1)
                if cm is not None:
                    cm.__exit__(None, None, None)
                mk.nwait = getattr(mk, "nwait", 0) + nwait

            @block.tensor
            def _(E):
                run("pe", E)

            @block.scalar
            def _(E):
                run("act", E)

            @block.vector
            def _(E):
                run("dve", E)

            @block.gpsimd
            def _(E):
                run("pool", E)

            @block.sync
            def _(E):
                run("sp", E)

D = 1024
DFF = 2816
NF = DFF // 128
T = 512
NB = T // 128
EPS = 1e-6
SCALE = 96 ** -0.5

CP_NORM = 0
CP_CONVW = 48
CP_CONVB = 176
CP_N = 208
RP_DTB = 0
RP_ALOG = 32
RP_D = 64
RP_SSMNW = 96
RP_QA = 2144
RP_KVA = 2528
RP_QN = 2784
RP_KN = 2880
RP_INVF = 2976
RP_N = 2992
CK_ID = 0
CK_TRIU = 128
CK_USTR = 256
CK_SEL = 384
CK_N = 416


class Builder:
    def __init__(self, n_seq=2, seq_len=4096, stop=None, do_prepass=True):
        self.n_seq = n_seq
        self.seq_len = seq_len
        self.stop = stop
        self.nt = seq_len // T
        nc = bass.Bass("TRN2", target_bir_lowering=False)
        self.nc = nc
        self.mk = MK(nc)
        self.es = ExitStack()

    def dram_in(self, name, shape, dt=F32):
        return self.nc.dram_tensor(name, list(shape), dt, kind="ExternalInput").ap()

    def dram_scr(self, name, shape, dt=BF16):
        return self.nc.dram_tensor(name, list(shape), dt, kind="Internal").ap()

    def alloc(self, fshape, dt, name=None):
        n = int(np.prod(fshape)) * ESZ[dt]
        n4 = (n + 3) // 4
        off = self.aoff
        self.aoff += (n4 + 7) // 8 * 8
        assert self.aoff <= self.acap, ("arena overflow", name, self.aoff, self.acap)
        return self.view(off, fshape, dt)

    def view(self, off_f32, fshape, dt):
        n = int(np.prod(fshape)) * ESZ[dt]
        n4 = (n + 3) // 4
        ap = self.arena[:, off_f32:off_f32 + n4]
        if dt != F32:
            ap = ap.bitcast(dt)
            ap = ap[:, 0:int(np.prod(fshape))]
        if len(fshape) == 2:
            ap = ap.rearrange("p (a b) -> p a b", a=fshape[0])
        elif len(fshape) == 3:
            ap = ap.rearrange("p (a b c) -> p a b c", a=fshape[0], b=fshape[1])
        return V(ap, "sb", off_f32 * 4, fshape, dt)

    def ring_alloc(self, fshape, dt):
        n = int(np.prod(fshape)) * ESZ[dt]
        n4 = ((n + 3) // 4 + 7) // 8 * 8
        if self.rpos + n4 > self.rcap:
            self.rpos = 0
        off = self.rbase + self.rpos
        self.rpos += n4
        return self.view(off, fshape, dt)

    def psum(self, bank, fshape, dt=F32, off=0):
        t = self.pbanks[bank]
        n = int(np.prod(fshape))
        if dt == F32:
            ap = t[:, off // 4: off // 4 + n]
        else:
            ap = t[:].bitcast(BF16)[:, off // 2: off // 2 + n]
        if len(fshape) == 2:
            ap = ap.rearrange("p (a b) -> p a b", a=fshape[0])
        elif len(fshape) == 3:
            ap = ap.rearrange("p (a b c) -> p a b c", a=fshape[0], b=fshape[1])
        v = V(ap, f"ps{bank}", off, fshape, dt)
        v.full_iv = (0, 2048)
        v.lo, v.hi = 0, 2048
        return v

    def dma(self, eng, out, in_, reads, writes, name=None):
        nb = ESZ.get(out.dtype, 4)
        for d_ in out.shape:
            nb *= d_
        self.mk.op(eng, lambda E: E.dma_start(out=out, in_=in_), reads=reads, writes=writes, ndma=1, name=name,
                   nbytes=nb)

    def mm_group(self, out, pairs, reads, name=None):
        oap = out.ap
        n = len(pairs)

        def fn(E):
            r = None
            for i, (l, rr) in enumerate(pairs):
                r = E.matmul(oap, l, rr, start=(i == 0), stop=(i == n - 1))
            return r
        nn = 1
        for d_ in out.fshape:
            nn *= d_
        self.mk.op("pe", fn, reads=reads, writes=[out], name=name, cost=n * (max(nn, 96) / 2.4 + 8.0))

    def transpose(self, out, in_, ident, name=None):
        self.mk.op("pe", lambda E: E.transpose(out.ap, in_.ap, ident.ap), reads=[in_, ident], writes=[out], name=name,
                   cost=70.0)

    def act(self, out, in_, func, bias=None, scale=None, accum=None, extra_reads=(), eng="act"):
        kw = {}
        reads = [in_] + list(extra_reads)
        writes = [out]
        if bias is not None:
            if isinstance(bias, V):
                kw["bias"] = bias.ap
                reads.append(bias)
            else:
                kw["bias"] = bias
        if scale is not None:
            if isinstance(scale, V):
                kw["scale"] = scale.ap
                reads.append(scale)
            else:
                kw["scale"] = scale
        if accum is not None:
            kw["accum_out"] = accum.ap
            writes.append(accum)
        oa, ia = out.ap, in_.ap
        self.mk.op("act", lambda E: E.activation(oa, ia, func, **kw), reads=reads, writes=writes)

    def tt(self, eng, out, a, b, op, a_ap=None, b_ap=None):
        aa = a.ap if a_ap is None else a_ap
        ba = b.ap if b_ap is None else b_ap
        oa = out.ap
        self.mk.op(eng, lambda E: E.tensor_tensor(oa, aa, ba, op), reads=[a, b], writes=[out])

    def ts(self, eng, out, a, s1, s2, op0, op1=None, a_ap=None):
        aa = a.ap if a_ap is None else a_ap
        oa = out.ap
        reads = [a]
        v1 = s1
        v2 = s2
        if isinstance(s1, V):
            reads.append(s1)
            v1 = s1.ap
        if isinstance(s2, V):
            reads.append(s2)
            v2 = s2.ap
        if op1 is None:
            self.mk.op(eng, lambda E: E.tensor_scalar(oa, aa, v1, None, op0), reads=reads, writes=[out])
        else:
            self.mk.op(eng, lambda E: E.tensor_scalar(oa, aa, v1, v2, op0, op1), reads=reads, writes=[out])

    def stt(self, eng, out, a, s, b, op0, op1, a_ap=None, b_ap=None):
        aa = a.ap if a_ap is None else a_ap
        ba = b.ap if b_ap is None else b_ap
        oa = out.ap
        reads = [a, b]
        sv = s
        if isinstance(s, V):
            reads.append(s)
            sv = s.ap
        self.mk.op(eng, lambda E: E.scalar_tensor_tensor(oa, aa, sv, ba, op0, op1), reads=reads, writes=[out])

    def copy(self, eng, out, in_, in_ap=None):
        ia = in_.ap if in_ap is None else in_ap
        oa = out.ap
        if eng == "act":
            self.mk.op("act", lambda E: E.copy(oa, ia), reads=[in_], writes=[out])
        else:
            self.mk.op(eng, lambda E: E.tensor_copy(oa, ia), reads=[in_], writes=[out])

    def memset(self, eng, out, val):
        oa = out.ap
        self.mk.op(eng, lambda E: E.memset(oa, val), reads=[], writes=[out])
import os as _os
def _b_setup(self):
    nc = self.nc
    S = self.seq_len
    ns = self.n_seq
    es = self.es
    self.x_fm = self.dram_in("x_fm", [ns, D, S])
    self.pos_l = self.dram_in("pos_l", [ns, 128, S // 128], I32)
    self.colpack_d = self.dram_in("colpack", [128, CP_N])
    self.rowpack_d = self.dram_in("rowpack", [128, RP_N])
    self.constpack_d = self.dram_in("constpack", [128, CK_N])
    self.out_fm = nc.dram_tensor("out_fm", [ns, D, S], F32, kind="ExternalOutput").ap()
    wspecs = {
        "ffn_g": [4 * NF * 128, 1024], "ffn_u": [4 * NF * 128, 1024], "ffn_d": [4 * NF * 2 * 128, 512],
        "ssm_win_z": [4 * 128, 8 * 512], "ssm_win_x": [32 * 128, 8 * 128], "ssm_win_dt": [128, 8 * 32],
        "ssm_wout": [8 * 128, 16 * 128],
        "mla_win": [128, 8 * 672], "mla_wqb": [128, 3 * 1536], "mla_wkvb": [128, 2 * 2048],
        "mla_wout": [8 * 128, 8 * 128],
    }
    self.w32 = {}
    self.w16 = {}
    for k, shp in wspecs.items():
        self.w32[k] = self.dram_in(k, shp)
        self.w16[k] = self.dram_scr(k + "_bf", shp)
    self.kc_d = self.dram_scr("kcache", [16, 96, S])
    self.vc_d = self.dram_scr("vcache", [16, 128, (S // 128) * 64])
    self.acap = 51680
    self.arena = es.enter_context(nc.sbuf_tensor("arena", [128, self.acap], F32))
    self.aoff = 0
    self.pbanks = [es.enter_context(nc.psum_tensor(f"psb{i}", [128, 512], F32)) for i in range(8)]
    self.colpack = self.alloc([CP_N], F32)
    self.rowpack = self.alloc([RP_N], F32)
    self.ck = self.alloc([CK_N], F32)
    self.ident_bf = self.alloc([128], BF16)
    self.triu_bf = self.alloc([128], BF16)
    self.ustr_bf = self.alloc([128], BF16)
    self.ones_bf = self.alloc([128], BF16)
    self.sel_bf = self.alloc([32], BF16)
    self.selneg_bf = self.alloc([32], BF16)
    self.mneg_bf = self.alloc([128], BF16)
    self.ones_d = self.alloc([128], BF16)
    self.a_row = self.alloc([32], F32)
    self.qw_row = self.alloc([96], F32)
    self.invn3 = self.alloc([3], F32)
    self.X = [self.alloc([8, T], F32), self.alloc([8, T], F32)]
    self.S_state = self.alloc([2048], F32)
    self.S_bf = self.alloc([2048], BF16)
    self.carry = self.alloc([32, 3], F32)
    self.posf = self.alloc([S // 128], F32)
    self.posi = self.alloc([S // 128], I32)
    self.rcap = 8192
    self.rbase = self.aoff
    self.aoff += self.rcap
    self.rpos = 0
    mk = self.mk
    self.dma("sp", self.colpack.ap, self.colpack_d, [], [self.colpack])
    self.dma("sp", self.rowpack.ap, self.rowpack_d, [], [self.rowpack])
    self.dma("sp", self.ck.ap, self.constpack_d, [], [self.ck])
    self.copy("dve", self.ident_bf, self.ck[:, CK_ID:CK_ID + 128])
    self.copy("dve", self.triu_bf, self.ck[:, CK_TRIU:CK_TRIU + 128])
    self.copy("dve", self.ustr_bf, self.ck[:, CK_USTR:CK_USTR + 128])
    self.memset("pool", self.ones_bf, 1.0)
    self.copy("dve", self.sel_bf, self.ck[:, CK_SEL:CK_SEL + 32])
    self.ts("dve", self.selneg_bf, self.ck[:, CK_SEL:CK_SEL + 32], -1.0, None, ALU.mult)
    self.ts("dve", self.mneg_bf, self.ck[:, CK_USTR:CK_USTR + 128], -60000.0, None, ALU.mult)
    self.memset("pool", self.ones_d, 1.0 / 1024)
    for i_, v_ in enumerate((1.0 / 384, 1.0 / 256, 1.0 / 32)):
        self.memset("pool", self.invn3[:, i_:i_ + 1], v_)
    self.act(self.a_row, self.rowpack[:, RP_ALOG:RP_ALOG + 32], AF.Exp)
    self.ts("dve", self.a_row, self.a_row, -1.0, None, ALU.mult)
    self.tt("dve", self.qw_row[:, 0:64], self.rowpack[:, RP_QN:RP_QN + 64], self.rowpack[:, RP_KN:RP_KN + 64], ALU.mult)
    self.copy("dve", self.qw_row[:, 64:96], self.rowpack[:, RP_QN + 64:RP_QN + 96])
    self.ident = self.ck[:, CK_ID:CK_ID + 128]
    self.triu = self.ck[:, CK_TRIU:CK_TRIU + 128]
    self.ustr = self.ck[:, CK_USTR:CK_USTR + 128]


def _b_prepass(self):
    def rows(k, fi=None):
        n = self.w32[k].shape[0]
        if fi is None:
            return (k, 0, n)
        per = n // 4
        return (k, fi * per, (fi + 1) * per)
    plan = [rows("ffn_g", 0), rows("ffn_u", 0), rows("ffn_d", 0),
            rows("ssm_win_z"), rows("ssm_win_dt"), rows("ssm_win_x"), rows("ssm_wout"),
            rows("ffn_g", 1), rows("ffn_u", 1), rows("ffn_d", 1),
            rows("ffn_g", 2), rows("ffn_u", 2), rows("ffn_d", 2),
            rows("mla_win"), rows("mla_wqb"), rows("mla_wkvb"), rows("mla_wout"),
            rows("ffn_g", 3), rows("ffn_u", 3), rows("ffn_d", 3)]
    for (k, a, b_) in plan:
        src = self.w32[k]
        dst = self.w16[k]
        cols = src.shape[1]
        step = max(128, (262144 // cols) // 128 * 128)
        for r0 in range(a, b_, step):
            r1 = min(b_, r0 + step)
            self.dma("pool", dst[r0:r1, :], src[r0:r1, :], [], [Key("w_" + k, r0, r1)], name="cvt")


def _wload(self, key, r0, fshape, eng="sp"):
    v = self.ring_alloc(fshape, BF16)
    src = self.w16[key][r0:r0 + 128, :]
    dst = v.ap
    if len(fshape) == 2:
        src = src.rearrange("p (a b) -> p a b", a=fshape[0])
    self.dma(eng, dst, src, [Key("w_" + key, r0, r0 + 128)], [v], name="wld")
    return v


def _b_rmsnorm_fm(self, X, ni, hT):
    mark = self.aoff
    sq = self.alloc([8, T], BF16)
    rstd = self.alloc([T], F32)
    for kc in range(8):
        self.act(sq[:, kc, :], X[:, kc, :], AF.Square)
    ps = self.psum(7, [T])
    self.mm_group(ps, [(self.ones_d.ap, sq[:, kc, :].ap) for kc in range(8)], reads=[self.ones_d, sq])
    self.rsqrt_eps(rstd, ps)
    for kc in range(8):
        c = CP_NORM + ni * 8 + kc
        self.stt("dve", hT[:, kc, :], X[:, kc, :], self.colpack[:, c:c + 1], rstd,
                 ALU.mult, ALU.mult)
    self.aoff = mark


def _b_rsqrt_eps(self, out, in_):
    self.ts("dve", out, in_, EPS, None, ALU.add)
    self.act(out, out, AF.Ln)
    self.act(out, out, AF.Exp, scale=-0.5)


def _b_ffn(self, X, fi, ni, slot=0):
    mark = self.aoff
    if slot:
        self.aoff += 13568
    hT = self.alloc([8, T], BF16)
    gT = self.alloc([NF, T], BF16)
    sg = [self.alloc([T], F32) for _ in range(3)]
    su = [self.alloc([T], F32) for _ in range(3)]
    rstd = self.alloc([T], F32)
    sq = self.alloc([8, T], BF16)
    for kc in range(8):
        c = CP_NORM + ni * 8 + kc
        self.act(hT[:, kc, :], X[:, kc, :], AF.Copy, scale=self.colpack[:, c:c + 1])
    for kc in range(8):
        self.act(sq[:, kc, :], X[:, kc, :], AF.Square)
    ps7 = self.psum(7, [T])
    self.mm_group(ps7, [(self.ones_d.ap, sq[:, kc, :].ap) for kc in range(8)], reads=[self.ones_d, sq])
    self.rsqrt_eps(rstd, ps7)
    for f in range(NF):
        r0 = (fi * NF + f) * 128
        wg = self.wload("ffn_g", r0, [8, 128])
        wu = self.wload("ffn_u", r0, [8, 128])
        pg = self.psum(f % 2, [T])
        pu = self.psum(2 + f % 2, [T])
        self.mm_group(pg, [(wg[:, kc, :].ap, hT[:, kc, :].ap) for kc in range(8)], reads=[wg, hT])
        self.mm_group(pu, [(wu[:, kc, :].ap, hT[:, kc, :].ap) for kc in range(8)], reads=[wu, hT])
        s = sg[f % 3]
        u2 = su[f % 3]
        self.tt("dve", s, pg, rstd, ALU.mult)
        self.tt("dve", u2, pu, rstd, ALU.mult)
        self.act(s, s, AF.Silu)
        self.tt("pool", gT[:, f, :], s, u2, ALU.mult)
    for half in range(2):
        pss = [self.psum(4 + c, [T]) for c in range(4)]
        for f in range(NF):
            r0 = ((fi * NF + f) * 2 + half) * 128
            wd = self.wload("ffn_d", r0, [512])
            g_ap = gT[:, f, :].ap
            w_aps = [wd[:, c * 128:(c + 1) * 128].ap for c in range(4)]
            p_aps = [p.ap for p in pss]

            def fn(E, w_aps=w_aps, g_ap=g_ap, p_aps=p_aps, f=f):
                r = None
                for c in range(4):
                    r = E.matmul(p_aps[c], w_aps[c], g_ap, start=(f == 0), stop=(f == NF - 1))
                return r
            self.mk.op("pe", fn, reads=[wd, gT[:, f, :]], writes=pss, cost=4 * (512 / 2.4 + 8))
        for c in range(4):
            dc = half * 4 + c
            self.stt("dve", X[:, dc, :], pss[c], 0.5, X[:, dc, :], ALU.mult, ALU.add)
    self.aoff = mark


def _b_main(self):
    self.setup()
    self.prepass()
    stop = self.stop
    tiles = [(b, t) for b in range(self.n_seq) for t in range(self.nt)]

    def xload(i):
        b_, t_ = tiles[i]
        src = self.x_fm[b_, :, t_ * T:(t_ + 1) * T].rearrange("(kc p) t -> p kc t", p=128)
        self.dma("sp", self.X[i % 2].ap, src, [], [self.X[i % 2]], name="xload")
    xload(0)
    for ti, (b, t) in enumerate(tiles):
        if t == 0:
            self.seq_begin(b)
        X = self.X[ti % 2]
        t0 = t * T
        phases = [
            lambda: self.ffn(X, 0, 0, slot=1),
            lambda: self.mamba(X, b, t),
            lambda: self.ffn(X, 1, 2),
            lambda: self.ffn(X, 2, 3),
            lambda: self.mla(X, b, t),
            lambda: self.ffn(X, 3, 5),
        ]
        for pi, ph in enumerate(phases):
            if stop is not None and pi >= stop:
                break
            self.mk.label = "p%d_t%d" % (pi, ti)
            ph()
            if pi == 3 and ti + 1 < len(tiles):
                self.mk.label = None
                xload(ti + 1)
        self.mk.label = None
        if stop is not None and stop <= 3 and ti + 1 < len(tiles):
            xload(ti + 1)
        dst = self.out_fm[b, :, t0:t0 + T].rearrange("(kc p) t -> p kc t", p=128)
        self.dma("sp", dst, X.ap, [X], [Key("out", ti, ti + 1)], name="xstore")
    ti = len(tiles)
    self.mk.op("sp", lambda E: None, reads=[Key("out", 0, ti)], writes=[])
    if _os.environ.get("KSCHED", "1") != "0":
        self.mk.schedule(window=int(_os.environ.get("KWIN", "64")))
    self.mk.emit()
    self.es.close()
    return self.nc


def _b_seq_begin(self, b):
    self.memset("pool", self.S_state, 0.0)
    self.memset("pool", self.S_bf, 0.0)
    self.memset("pool", self.carry, 0.0)
    self.dma("sp", self.posi.ap, self.pos_l[b], [], [self.posi])
    self.copy("dve", self.posf, self.posi)


Builder.setup = _b_setup
Builder.prepass = _b_prepass
Builder.wload = _wload
Builder.rmsnorm_fm = _b_rmsnorm_fm
Builder.ffn = _b_ffn
Builder.rsqrt_eps = _b_rsqrt_eps
Builder.main = _b_main
Builder.seq_begin = _b_seq_begin
def _bc(v, dims):
    a = v.ap
    return bass.AP(tensor=a.tensor, offset=a.offset, ap=[list(a.ap[0])] + [list(d) for d in dims])


_DBG = int(_os.environ.get('KDBG', '0'))


def _b_mamba(self, X, b, t):
    mark = self.aoff
    mk = self.mk
    zs = self.alloc([NB, 2048], BF16)
    x_tm = self.alloc([NB, 2048], BF16)
    B_tm = self.alloc([NB, 1024], BF16)
    BC = self.alloc([16, T], BF16)
    dt = self.alloc([NB, 32], F32)
    dtA3 = self.alloc([NB, 3, 32], BF16)
    gT = self.alloc([16, T], BF16)
    mark2 = self.aoff
    hT = self.alloc([8, T], BF16)
    self.rmsnorm_fm(X, 1, hT)
    rp = self.rowpack
    cp = self.colpack
    for cg in range(4):
        w = self.wload("ssm_win_z", cg * 128, [8, 512])
        for blk in range(NB):
            ps = self.psum(blk % 2, [512])
            self.mm_group(ps, [(hT[:, kc, blk * 128:(blk + 1) * 128].ap, w[:, kc, :].ap) for kc in range(8)],
                          reads=[w, hT])
            self.act(zs[:, blk, cg * 512:(cg + 1) * 512], ps, AF.Silu)
    if _DBG == 1:
        self.aoff = mark
        return
    wdt = self.wload("ssm_win_dt", 0, [8, 32])
    tmp = self.alloc([6, 32], F32)
    for blk in range(NB):
        ps = self.psum(2, [32], off=blk * 128)
        self.mm_group(ps, [(hT[:, kc, blk * 128:(blk + 1) * 128].ap, wdt[:, kc, :].ap) for kc in range(8)],
                      reads=[wdt, hT])
        xb, nx, ax, e, l, r = (tmp[:, i, :] for i in range(6))
        self.tt("dve", xb, ps, rp[:, RP_DTB:RP_DTB + 32], ALU.add)
        self.ts("dve", nx, xb, -1.0, None, ALU.mult)
        self.tt("dve", ax, xb, nx, ALU.max)
        self.act(e, ax, AF.Exp, scale=-1.0)
        self.ts("dve", e, e, 1.0, None, ALU.add)
        self.act(l, e, AF.Ln)
        self.ts("dve", r, xb, 0.0, None, ALU.max)
        d = dt[:, blk, :]
        self.tt("dve", d, r, l, ALU.add)
        dA, r1, r2 = tmp[:, 0, :], tmp[:, 1, :], tmp[:, 2, :]
        self.tt("dve", dA, d, self.a_row, ALU.mult)
        p1, p2, p3 = (dtA3[:, blk, j, :] for j in range(3))
        self.copy("dve", p1, dA)
        self.tt("dve", r1, dA, p1, ALU.subtract)
        self.copy("dve", p2, r1)
        self.tt("dve", r2, r1, p2, ALU.subtract)
        self.copy("dve", p3, r2)
    if _DBG == 2:
        self.aoff = mark
        return
    pre = [self.alloc([T + 3], F32), self.alloc([T + 3], F32)]
    acc = [self.alloc([T], F32), self.alloc([T], F32)]
    xst = [self.alloc([T], BF16), self.alloc([T], BF16)]
    for cc in range(32):
        w = self.wload("ssm_win_x", cc * 128, [8, 128])
        ps = self.psum(cc % 2, [T])
        self.mm_group(ps, [(w[:, kc, :].ap, hT[:, kc, :].ap) for kc in range(8)], reads=[w, hT])
        p = pre[cc % 2]
        a = acc[cc % 2]
        self.copy("pool", p[:, 0:3], self.carry[:, cc, :])
        self.copy("act", p[:, 3:T + 3], ps)
        cw = CP_CONVW + cc * 4
        self.ts("dve", a, p[:, 0:T], cp[:, cw:cw + 1], cp[:, CP_CONVB + cc:CP_CONVB + cc + 1], ALU.mult, ALU.add)
        for k in range(1, 4):
            self.stt("dve", a, p[:, k:k + T], cp[:, cw + k:cw + k + 1], a, ALU.mult, ALU.add)
        self.copy("pool", self.carry[:, cc, :], p[:, T:T + 3])
        if cc < 16:
            dst = xst[cc % 2]
        else:
            dst = BC[:, cc - 16, :]
        self.act(dst, a, AF.Silu)
        if cc < 24:
            pt = self.psum(2 + cc % 2, [NB, 128], BF16)
            for blk in range(NB):
                self.transpose(pt[:, blk, :], dst[:, blk * 128:(blk + 1) * 128], self.ident_bf)
            if cc < 16:
                self.copy("act", x_tm[:, :, cc * 128:(cc + 1) * 128], pt)
            else:
                self.copy("act", B_tm[:, :, (cc - 16) * 128:(cc - 15) * 128], pt)
    if _DBG == 3:
        self.aoff = mark
        return
    self.mk.label = (self.mk.label or "") + "_ssd"
    self.aoff = mark2
    sm = self.alloc([8, 32], F32)
    xdt_l = [self.alloc([2048], BF16)] * 2
    xw = [self.alloc([256], BF16) for _ in range(2)]
    CBm = [self.alloc([4, 128], F32), self.alloc([4, 128], F32)]
    A3 = self.alloc([3, 32], BF16)
    acsT = self.alloc([128], BF16)
    a3t = self.alloc([2, 32], F32)
    Eb = [self.alloc([4, 128], F32) for _ in range(2)]
    MT = [self.alloc([4, 128], BF16) for _ in range(3)]
    Y = self.alloc([2048], F32)
    tq = [self.alloc([256], F32) for _ in range(3)]
    uq = [self.alloc([256], F32) for _ in range(2)]
    junk = self.alloc([256], F32)
    ss = self.alloc([8], F32)
    gn = self.alloc([2048], BF16)
    for blk in range(NB):
        acs, eacs, lastb, dlast, dte = (sm[:, i, :] for i in range(5))
        xdt = xdt_l[blk % 2]
        parts = [dtA3[:, blk, j, :] for j in range(3)]
        ps_a = self.psum(7, [32], off=0)
        self.mm_group(ps_a, [(self.triu_bf.ap, pj.ap) for pj in parts], reads=[self.triu_bf] + parts)
        self.copy("dve", acs, ps_a)
        self.act(eacs, ps_a, AF.Exp)
        q1, q2, q3 = A3[:, 0, :], A3[:, 1, :], A3[:, 2, :]
        e1, e2 = a3t[:, 0, :], a3t[:, 1, :]
        self.copy("dve", q1, acs)
        self.tt("dve", e1, acs, q1, ALU.subtract)
        self.copy("dve", q2, e1)
        self.tt("dve", e2, e1, q2, ALU.subtract)
        self.copy("dve", q3, e2)
        pta = self.psum(6, [4, 128], BF16)
        self.transpose(pta[0:96, 0, :], self._flat(A3, 96), self.ident_bf)
        self.copy("act", acsT[0:96], pta[0:96, 0, :])
        ps_l = self.psum(7, [32], off=256)
        self.mm_group(ps_l, [(self.ones_bf.ap, pj.ap) for pj in parts], reads=[self.ones_bf] + parts)
        self.act(dlast, ps_l, AF.Exp)
        self.tt("dve", dte, ps_l, acs, ALU.subtract)
        self.act(dte, dte, AF.Exp)
        d = dt[:, blk, :]
        xa = x_tm[:, blk, :]
        self.mk.op("pool", lambda E, o=xdt.ap.rearrange("p (h d) -> p h d", h=32),
                   i0=xa.ap.rearrange("p (h d) -> p h d", h=32), i1=_bc(d, [[1, 32], [0, 64]]):
                   E.tensor_tensor(o, i0, i1, ALU.mult), reads=[xa, d], writes=[xdt])
        for half in range(2):
            ps_cb = self.psum(half, [4, 128])
            for gg in range(4):
                g = half * 4 + gg
                self.mm_group(ps_cb[:, gg, :], [(BC[:, g, blk * 128:(blk + 1) * 128].ap,
                                                 BC[:, 8 + g, blk * 128:(blk + 1) * 128].ap)],
                              reads=[BC[:, g, blk * 128:(blk + 1) * 128], BC[:, 8 + g, blk * 128:(blk + 1) * 128]])
            self.mk.op("dve", lambda E, o=CBm[half].ap, i0=ps_cb.ap, i1=_bc(self.triu, [[0, 4], [1, 128]]):
                       E.tensor_tensor(o, i0, i1, ALU.mult), reads=[ps_cb, self.triu], writes=[CBm[half]])
        self.memset("pool", ss, 0.0)
        for g in range(8):
            k = g % 2
            k3 = g % 3
            ps_s = self.psum(2 + k, [4, 128])
            for hh in range(4):
                h = g * 4 + hh
                sel_l = _bc(self.sel_bf[0:96, h:h + 1], [[0, 128]])
                sel_r = _bc(self.selneg_bf[0:96, h:h + 1], [[0, 128]])
                self.mm_group(ps_s[:, hh, :], [(sel_l, acsT[0:96].ap), (acsT[0:96].ap, sel_r),
                                               (self.ident_bf.ap, self.mneg_bf.ap)],
                              reads=[acsT, self.sel_bf, self.selneg_bf, self.ident_bf, self.mneg_bf])
            self.act(Eb[k], ps_s, AF.Exp)
            cb = CBm[g // 4][:, g % 4, :]
            self.mk.op("dve", lambda E, o=MT[k3].ap, i0=Eb[k].ap, i1=_bc(cb, [[0, 4], [1, 128]]):
                       E.tensor_tensor(o, i0, i1, ALU.mult), reads=[Eb[k], cb], writes=[MT[k3]])
            ps_y = self.psum(4 + k, [512])
            for hh in range(4):
                h = g * 4 + hh
                self.mm_group(ps_y[:, hh * 64:(hh + 1) * 64], [(MT[k3][:, hh, :].ap, xdt[:, h * 64:(h + 1) * 64].ap)],
                              reads=[MT[k3], xdt[:, h * 64:(h + 1) * 64]])
            Cg = BC[:, 8 + g, blk * 128:(blk + 1) * 128]
            Sg = self.S_bf[:, g * 256:(g + 1) * 256]
            self.mm_group(ps_y[:, 256:512], [(Cg.ap, Sg.ap)], reads=[Cg, Sg])
            ea = eacs[:, g * 4:(g + 1) * 4]
            self.mk.op("dve", lambda E, o=tq[k3].ap.rearrange("p (h d) -> p h d", h=4),
                       i0=ps_y[:, 256:512].ap.rearrange("p (h d) -> p h d", h=4), i1=_bc(ea, [[1, 4], [0, 64]]):
                       E.tensor_tensor(o, i0, i1, ALU.mult), reads=[ps_y[:, 256:512], ea], writes=[tq[k3]])
            xg = x_tm[:, blk, g * 256:(g + 1) * 256]
            Dr = rp[:, RP_D + g * 4:RP_D + g * 4 + 4]
            self.mk.op("pool", lambda E, o=uq[k].ap.rearrange("p (h d) -> p h d", h=4),
                       i0=xg.ap.rearrange("p (h d) -> p h d", h=4), i1=_bc(Dr, [[1, 4], [0, 64]]):
                       E.tensor_tensor(o, i0, i1, ALU.mult), reads=[xg, Dr], writes=[uq[k]])
            Yg = Y[:, g * 256:(g + 1) * 256]
            self.tt("dve", Yg, ps_y[:, 0:256], tq[k3], ALU.add)
            self.tt("pool", Yg, Yg, uq[k], ALU.add)
            self.tt("pool", Yg, Yg, zs[:, blk, g * 256:(g + 1) * 256], ALU.mult)
            self.act(junk, Yg, AF.Square, accum=ss[:, g:g + 1])
            de = dte[:, g * 4:(g + 1) * 4]
            xdg = xdt[:, g * 256:(g + 1) * 256]
            self.mk.op("pool", lambda E, o=xw[k].ap.rearrange("p (h d) -> p h d", h=4),
                       i0=xdg.ap.rearrange("p (h d) -> p h d", h=4), i1=_bc(de, [[1, 4], [0, 64]]):
                       E.tensor_tensor(o, i0, i1, ALU.mult), reads=[xdg, de], writes=[xw[k]])
            ps_st = self.psum(6, [256], off=k * 1024)
            Bg = B_tm[:, blk, g * 128:(g + 1) * 128]
            self.mm_group(ps_st, [(Bg.ap, xw[k].ap)], reads=[Bg, xw[k]])
            Sf = self.S_state[:, g * 256:(g + 1) * 256]
            dl = dlast[:, g * 4:(g + 1) * 4]
            self.mk.op("dve", lambda E, o=Sf.ap.rearrange("p (h d) -> p h d", h=4),
                       i0=Sf.ap.rearrange("p (h d) -> p h d", h=4), i1=_bc(dl, [[1, 4], [0, 64]]):
                       E.tensor_tensor(o, i0, i1, ALU.mult), reads=[Sf, dl], writes=[Sf])
            self.tt("dve", Sf, ps_st, Sf, ALU.add)
            self.copy("pool", Sg, Sf)
        self.ts("dve", ss, ss, 1.0 / 256, EPS, ALU.mult, ALU.add)
        self.act(ss, ss, AF.Ln)
        self.act(ss, ss, AF.Exp, scale=-0.5)
        for g in range(8):
            self.stt("dve", gn[:, g * 256:(g + 1) * 256], Y[:, g * 256:(g + 1) * 256], ss[:, g:g + 1],
                     rp[:, RP_SSMNW + g * 256:RP_SSMNW + (g + 1) * 256], ALU.mult, ALU.mult)
        for q in range(4):
            pt = self.psum(q % 2, [4, 128], BF16)
            for c4 in range(4):
                cc = q * 4 + c4
                self.transpose(pt[:, c4, :], gn[:, cc * 128:(cc + 1) * 128], self.ident_bf)
            self.copy("act", gT[:, q * 4:(q + 1) * 4, blk * 128:(blk + 1) * 128], pt)
    for dc in range(8):
        w = self.wload("ssm_wout", dc * 128, [16, 128])
        ps = self.psum(2 + dc % 2, [T])
        self.mm_group(ps, [(w[:, cc, :].ap, gT[:, cc, :].ap) for cc in range(16)], reads=[w, gT])
        self.tt("dve", X[:, dc, :], ps, X[:, dc, :], ALU.add)
    self.aoff = mark


Builder.mamba = _b_mamba
TWO_PI_HI = 6.28125
TWO_PI_LO = 0.0019353071795864769
PI = 3.141592653589793


def _b_bt(self, eng, out, a, b_ap_v, dims, op, a3=None, o3=None):
    oa = out.ap if o3 is None else o3
    aa = a.ap if a3 is None else a3
    ba = _bc(b_ap_v, dims)
    self.mk.op(eng, lambda E: E.tensor_tensor(oa, aa, ba, op), reads=[a, b_ap_v], writes=[out])


def _b_sincos(self, blk_g, cosv, sinv, sm):
    ang, kf, r, m, r2 = (sm[:, i, :] for i in range(5))
    ki = self.sc_ki
    rp = self.rowpack
    self.ts("dve", ang, rp[:, RP_INVF:RP_INVF + 16], self.posf[:, blk_g:blk_g + 1], None, ALU.mult)
    self.ts("dve", kf, ang, 1.0 / (2 * PI), None, ALU.mult)
    self.copy("dve", ki, kf)
    self.copy("dve", kf, ki)
    self.stt("dve", r, kf, -TWO_PI_HI, ang, ALU.mult, ALU.add)
    self.stt("dve", r, kf, -TWO_PI_LO, r, ALU.mult, ALU.add)

    def fold(dst, src):
        self.ts("dve", m, src, PI, -2 * PI, ALU.is_gt, ALU.mult)
        self.tt("dve", dst, src, m, ALU.add)
        self.ts("dve", m, dst, -PI, 2 * PI, ALU.is_lt, ALU.mult)
        self.tt("dve", dst, dst, m, ALU.add)
    fold(r2, r)
    self.act(sinv, r2, AF.Sin)
    self.ts("dve", r, r2, PI / 2, None, ALU.add)
    fold(r2, r)
    self.act(cosv, r2, AF.Sin)


def _b_rope(self, eng, out1, out2, x1, x2, cosv, sinv, tmp4, nh):
    a, b, c, d = tmp4
    dims = [[0, nh], [1, 16]]
    for (dst, src, cs) in ((a, x1, cosv), (b, x2, sinv), (c, x2, cosv), (d, x1, sinv)):
        self.bt("dve", dst, src, cs, dims, ALU.mult)
    self.tt("dve", out1, a, b, ALU.subtract)
    self.tt("dve", out2, c, d, ALU.add)


def _b_mla(self, X, b, t):
    mark = self.aoff
    rp = self.rowpack
    t0 = t * T
    kb0 = t0 // 128
    hT = self.alloc([8, T], BF16)
    self.rmsnorm_fm(X, 4, hT)
    QT = self.alloc([16, T], BF16)
    KT = self.alloc([16, T], BF16)
    Vst = self.alloc([16, NB, 64], BF16)
    mark2 = self.aoff
    proj = self.alloc([672], F32)
    ss3 = self.alloc([3], F32)
    cqn = self.alloc([384], BF16)
    ckvn = self.alloc([256], BF16)
    krn = self.alloc([32], F32)
    cqT = self.alloc([3, 128], BF16)
    ckvT = self.alloc([2, 128], BF16)
    Q32 = self.alloc([16, 96], F32)
    KV32 = self.alloc([16, 128], F32)
    SQ = self.alloc([16, 96], F32)
    tn = self.alloc([16, 64], F32)
    u = self.alloc([16, 32], F32)
    t4 = [self.alloc([16, 16], F32) for _ in range(4)]
    k4 = [self.alloc([1, 16], F32) for _ in range(4)]
    kro = self.alloc([32], BF16)
    ssm = self.alloc([4, 16], F32)
    scs = self.alloc([5, 16], F32)
    cosv = self.alloc([16], F32)
    sinv = self.alloc([16], F32)
    self.sc_ki = self.alloc([16], I32)
    Qtm = self.alloc([16, 96], BF16)
    Ktm = self.alloc([16, 96], BF16)
    w_in = self.wload("mla_win", 0, [8, 672])
    w_qb = self.wload("mla_wqb", 0, [3, 1536])
    w_kvb = self.wload("mla_wkvb", 0, [2, 2048])
    for blk in range(NB):
        tok = slice(blk * 128, (blk + 1) * 128)
        psA = self.psum(0, [512])
        psB = self.psum(1, [160])
        self.mm_group(psA, [(hT[:, kc, tok].ap, w_in[:, kc, 0:512].ap) for kc in range(8)], reads=[hT, w_in])
        self.mm_group(psB, [(hT[:, kc, tok].ap, w_in[:, kc, 512:672].ap) for kc in range(8)], reads=[hT, w_in])
        self.copy("act", proj[:, 0:512], psA)
        self.copy("act", proj[:, 512:672], psB)
        self.memset("pool", ss3, 0.0)
        for i, (lo, hi) in enumerate(((0, 384), (384, 640), (640, 672))):
            self.act(self._flat(SQ, hi - lo), proj[:, lo:hi], AF.Square, accum=ss3[:, i:i + 1])
        self.tt("dve", ss3, ss3, self.invn3, ALU.mult)
        self.rsqrt_eps(ss3, ss3)
        self.stt("dve", cqn, proj[:, 0:384], ss3[:, 0:1], rp[:, RP_QA:RP_QA + 384], ALU.mult, ALU.mult)
        self.stt("dve", ckvn, proj[:, 384:640], ss3[:, 1:2], rp[:, RP_KVA:RP_KVA + 256], ALU.mult, ALU.mult)
        self.stt("dve", krn, proj[:, 640:672], ss3[:, 2:3], rp[:, RP_KN + 64:RP_KN + 96], ALU.mult, ALU.mult)
        pt = self.psum(2, [4, 128], BF16)
        for i in range(3):
            self.transpose(pt[:, i, :], cqn[:, i * 128:(i + 1) * 128], self.ident_bf)
        self.copy("act", cqT, pt[:, 0:3, :])
        pt2 = self.psum(3, [4, 128], BF16)
        for i in range(2):
            self.transpose(pt2[:, i, :], ckvn[:, i * 128:(i + 1) * 128], self.ident_bf)
        self.copy("act", ckvT, pt2[:, 0:2, :])
        Qf = self._flat(Q32, 1536)
        for n in range(3):
            ps = self.psum(4 + n % 2, [512])
            self.mm_group(ps, [(cqT[:, kc, :].ap, w_qb[:, kc, n * 512:(n + 1) * 512].ap) for kc in range(3)],
                          reads=[cqT, w_qb])
            self.copy("act", Qf[:, n * 512:(n + 1) * 512], ps)
        KVf = self._flat(KV32, 2048)
        for n in range(4):
            ps = self.psum(6 + n % 2, [512])
            self.mm_group(ps, [(ckvT[:, kc, :].ap, w_kvb[:, kc, n * 512:(n + 1) * 512].ap) for kc in range(2)],
                          reads=[ckvT, w_kvb])
            self.copy("act", KVf[:, n * 512:(n + 1) * 512], ps)
        self.sincos(kb0 + blk, cosv, sinv, scs)
        self.act(SQ, Q32, AF.Square)
        ssn, ssr, ssk = ssm[:, 0, :], ssm[:, 1, :], ssm[:, 2, :]
        self.mk.op("dve", lambda E, o=ssn.ap, i=SQ[:, :, 0:64].ap: E.tensor_reduce(o, i, AX.X, ALU.add),
                   reads=[SQ], writes=[ssn])
        self.mk.op("dve", lambda E, o=ssr.ap, i=SQ[:, :, 64:96].ap: E.tensor_reduce(o, i, AX.X, ALU.add),
                   reads=[SQ], writes=[ssr])
        self.ts("dve", ssn, ssn, 1.0 / 64, None, ALU.mult)
        self.ts("dve", ssr, ssr, 1.0 / 32, None, ALU.mult)
        SQK = SQ[:, :, 0:64]
        self.act(SQK, KV32[:, :, 0:64], AF.Square)
        self.mk.op("dve", lambda E, o=ssk.ap, i=SQK.ap: E.tensor_reduce(o, i, AX.X, ALU.add),
                   reads=[SQK], writes=[ssk])
        self.ts("dve", ssk, ssk, 1.0 / 64, None, ALU.mult)
        r3 = self._flat(ssm, 64)[:, 0:48]
        self.rsqrt_eps(r3, r3)
        self.bt("dve", tn, Q32[:, :, 0:64], ssn, [[1, 16], [0, 64]], ALU.mult)
        self.bt("dve", Qtm[:, :, 0:64], tn, self.qw_row[:, 0:64], [[0, 16], [1, 64]], ALU.mult)
        self.bt("dve", u, Q32[:, :, 64:96], ssr, [[1, 16], [0, 32]], ALU.mult)
        self.bt("dve", u, u, self.qw_row[:, 64:96], [[0, 16], [1, 32]], ALU.mult)
        self.rope("dve", Qtm[:, :, 64:80], Qtm[:, :, 80:96], u[:, :, 0:16], u[:, :, 16:32], cosv, sinv, t4, 16)
        self.bt("dve", Ktm[:, :, 0:64], KV32[:, :, 0:64], ssk, [[1, 16], [0, 64]], ALU.mult)
        kr3 = self._as3(krn, 1, 32)
        ko3 = self._as3(kro, 1, 32)
        self.rope("dve", ko3[:, :, 0:16], ko3[:, :, 16:32], kr3[:, :, 0:16], kr3[:, :, 16:32], cosv, sinv, k4, 1)
        self.mk.op("pool", lambda E, o=Ktm[:, :, 64:96].ap, i=_bc(kro, [[0, 16], [1, 32]]): E.tensor_copy(o, i),
                   reads=[kro], writes=[Ktm[:, :, 64:96]])
        self.copy("act", Vst[:, :, blk, :], KV32[:, :, 64:128])
        for q4 in range(4):
            for (src, dstT, bank) in ((Qtm, QT, 2), (Ktm, KT, 3)):
                ptq = self.psum(bank, [4, 128], BF16)
                for i in range(4):
                    h = q4 * 4 + i
                    self.transpose(ptq[0:96, i, :], src[:, h, :], self.ident_bf)
                self.copy("act", dstT[0:96, q4 * 4:(q4 + 1) * 4, tok], ptq[0:96])
    kdst = self.kc_d[:, :, t0:t0 + T].rearrange("h d t -> d h t")
    self.dma("sp", kdst, KT.ap[0:96], [KT], [Key("kc", t0, t0 + T)], name="kst")
    vdst = self.vc_d[:, :, kb0 * 64:(kb0 + NB) * 64].rearrange("h p x -> p h x")
    self.dma("sp", vdst, Vst.ap.rearrange("p h b d -> p h (b d)"), [Vst], [Key("vc", t0, t0 + T)], name="vst")
    self.mk.label = (self.mk.label or "") + "_attn"
    self.aoff = mark2
    nkb = kb0 + NB
    nkeys = nkb * 128
    KH = [self.alloc([nkeys], BF16), self.alloc([nkeys], BF16)]
    VA = [self.alloc([nkb, 128], BF16), self.alloc([nkb, 128], BF16)]
    PT = [self.alloc([T], BF16), self.alloc([T], BF16), self.alloc([T], BF16)]
    OT = self.alloc([8, T], BF16)
    rden = [self.alloc([T], F32), self.alloc([T], F32)]
    self.memset("pool", VA[0][:, :, 64:128], 1.0)
    self.memset("pool", VA[1][:, :, 0:64], 1.0)
    pi = 0
    for h in range(16):
        par = h % 2
        kh = KH[par]
        va = VA[par]
        self.dma("sp", kh.ap[0:96], self.kc_d[h, :, 0:nkeys], [Key("kc", 0, nkeys)], [kh], name="kld")
        vsrc = self.vc_d[h, :, 0:nkb * 64].rearrange("p (kb d) -> p kb d", d=64)
        vdst_v = va[:, :, 0:64] if par == 0 else va[:, :, 64:128]
        self.dma("sp", vdst_v.ap, vsrc, [Key("vc", 0, nkeys)], [vdst_v], name="vld")
        ps_o = self.psum(2 + par, [T])
        ps_d = self.psum(4 + par, [T])
        for kb in range(nkb):
            j = kb - kb0
            q0 = 0 if j <= 0 else j * 128
            N = T - q0
            ps_s = self.psum(kb % 2, [T])
            self.mm_group(ps_s[:, 0:N], [(kh[0:96, kb * 128:(kb + 1) * 128].ap, QT[0:96, h, q0:T].ap)],
                          reads=[kh[:, kb * 128:(kb + 1) * 128], QT[:, h, :]])
            p = PT[pi % 3]
            pi += 1
            self.act(p[:, 0:N], ps_s[:, 0:N], AF.Exp, scale=SCALE)
            if j >= 0:
                self.tt("pool", p[:, 0:128], p[:, 0:128], self.triu_bf, ALU.mult)
            o_ap = ps_o[:, q0:T].ap
            d_ap = ps_d[:, q0:T].ap
            v_ap = va[:, kb, :].ap
            p_ap = p[:, 0:N].ap
            one_ap = self.ones_bf.ap

            def fn(E, o_ap=o_ap, d_ap=d_ap, v_ap=v_ap, p_ap=p_ap, one_ap=one_ap, kb=kb, nkb=nkb):
                return E.matmul(o_ap, v_ap, p_ap, start=(kb == 0), stop=(kb == nkb - 1))
            self.mk.op("pe", fn, reads=[va[:, kb, :], p[:, 0:N]], writes=[ps_o], cost=(N / 2.4 + 8))
        rows = slice(par * 64, par * 64 + 64)
        orows = slice((1 - par) * 64, (1 - par) * 64 + 64)
        rd = rden[par]
        self.mk.op("dve", lambda E, o=rd[orows].ap, i=ps_o[orows].ap: E.reciprocal(o, i), reads=[ps_o], writes=[rd])
        self.tt("dve", OT[rows, h // 2, :], ps_o[rows], rd[orows], ALU.mult)
    for dc in range(8):
        w = self.wload("mla_wout", dc * 128, [8, 128])
        ps = self.psum(6 + dc % 2, [T])
        self.mm_group(ps, [(w[:, c, :].ap, OT[:, c, :].ap) for c in range(8)], reads=[w, OT])
        self.tt("dve", X[:, dc, :], ps, X[:, dc, :], ALU.add)
    self.aoff = mark


def _b_flat(self, v, n):
    return self.view(v.base // 4, [n], v.dtype)


def _b_as3(self, v, a, b_):
    return self.view(v.base // 4, [a, b_], v.dtype)


Builder.mla = _b_mla
Builder.bt = _b_bt
Builder.sincos = _b_sincos
Builder.rope = _b_rope
Builder._flat = _b_flat
Builder._as3 = _b_as3
def _prep_shared(inp):
    f = np.float32
    sh = {}
    g = inp["ffn_w_gate"].reshape(4, 8, 128, NF, 128)
    sh["ffn_g"] = np.ascontiguousarray(g.transpose(0, 3, 2, 1, 4)).reshape(4 * NF * 128, 1024)
    u = inp["ffn_w_up"].reshape(4, 8, 128, NF, 128)
    sh["ffn_u"] = np.ascontiguousarray(u.transpose(0, 3, 2, 1, 4)).reshape(4 * NF * 128, 1024)
    d = inp["ffn_w_down"].reshape(4, NF, 128, 2, 512)
    sh["ffn_d"] = np.ascontiguousarray(d.transpose(0, 1, 3, 2, 4)).reshape(4 * NF * 2 * 128, 512)
    win = inp["ssm_w_in"][0]
    z = win[:, :2048].reshape(8, 128, 4, 512)
    sh["ssm_win_z"] = np.ascontiguousarray(z.transpose(2, 1, 0, 3)).reshape(4 * 128, 8 * 512)
    xb = win[:, 2048:6144].reshape(8, 128, 32, 128)
    sh["ssm_win_x"] = np.ascontiguousarray(xb.transpose(2, 1, 0, 3)).reshape(32 * 128, 8 * 128)
    dtw = win[:, 6144:6176].reshape(8, 128, 32)
    sh["ssm_win_dt"] = np.ascontiguousarray(dtw.transpose(1, 0, 2)).reshape(128, 8 * 32)
    wo = inp["ssm_w_out"][0].reshape(16, 128, 8, 128)
    sh["ssm_wout"] = np.ascontiguousarray(wo.transpose(2, 1, 0, 3)).reshape(8 * 128, 16 * 128)
    mi = inp["mla_w_in"][0].reshape(8, 128, 672)
    sh["mla_win"] = np.ascontiguousarray(mi.transpose(1, 0, 2)).reshape(128, 8 * 672)
    qb = inp["mla_w_q_b"][0].reshape(3, 128, 1536)
    sh["mla_wqb"] = np.ascontiguousarray(qb.transpose(1, 0, 2)).reshape(128, 3 * 1536)
    kvb = inp["mla_w_kv_b"][0].reshape(2, 128, 2048)
    sh["mla_wkvb"] = np.ascontiguousarray(kvb.transpose(1, 0, 2)).reshape(128, 2 * 2048)
    mo = inp["mla_w_out"][0].reshape(8, 128, 8, 128)
    sh["mla_wout"] = np.ascontiguousarray(mo.transpose(2, 1, 0, 3)).reshape(8 * 128, 8 * 128)
    cp = np.zeros((128, CP_N), f)
    cp[:, CP_NORM:CP_NORM + 48] = inp["norm_w"].reshape(6, 8, 128).transpose(2, 0, 1).reshape(128, 48)
    cp[:, CP_CONVW:CP_CONVW + 128] = inp["ssm_conv_w"][0].reshape(4, 32, 128).transpose(2, 1, 0).reshape(128, 128)
    cp[:, CP_CONVB:CP_CONVB + 32] = inp["ssm_conv_b"][0].reshape(32, 128).T
    sh["colpack"] = cp
    rp = np.zeros((128, RP_N), f)

    def rep(off, v):
        rp[:, off:off + v.size] = np.broadcast_to(v.reshape(1, -1), (128, v.size))
    rep(RP_DTB, inp["ssm_dt_bias"][0])
    rep(RP_ALOG, inp["ssm_a_log"][0])
    rep(RP_D, inp["ssm_d"][0])
    rep(RP_SSMNW, inp["ssm_norm_w"][0])
    rep(RP_QA, inp["mla_q_a_norm"][0])
    rep(RP_KVA, inp["mla_kv_a_norm"][0])
    rep(RP_QN, inp["mla_q_norm"][0])
    rep(RP_KN, inp["mla_k_norm"][0])
    rep(RP_INVF, (1.0 / (10000.0 ** (np.arange(0, 32, 2, dtype=f) / f(32)))).astype(f))
    sh["rowpack"] = rp
    ck = np.zeros((128, CK_N), f)
    ck[:, CK_ID:CK_ID + 128] = np.eye(128, dtype=f)
    k = np.arange(128)
    ck[:, CK_TRIU:CK_TRIU + 128] = (k[:, None] <= k[None, :]).astype(f)
    ck[:, CK_USTR:CK_USTR + 128] = (k[:, None] > k[None, :]).astype(f)
    ck[:96, CK_SEL:CK_SEL + 32] = (k[:96, None] % 32 == np.arange(32)[None, :]).astype(f)
    sh["constpack"] = ck
    return sh


_NC_CACHE = {}


def run_cores(inp, n_cores, n_seq, seq_len, stop=None, trace=False):
    key = (n_seq, seq_len, stop)
    sh = _prep_shared(inp)
    x = inp["x"]
    pos = inp["positions"]
    in_maps = []
    for c in range(n_cores):
        m = dict(sh)
        xs = x[c * n_seq:(c + 1) * n_seq, :seq_len, :]
        m["x_fm"] = np.ascontiguousarray(xs.transpose(0, 2, 1))
        ps = pos[c * n_seq:(c + 1) * n_seq, :seq_len].astype(np.int32)
        m["pos_l"] = np.ascontiguousarray(ps.reshape(n_seq, seq_len // 128, 128).transpose(0, 2, 1))
        in_maps.append(m)
    bld = Builder(n_seq=n_seq, seq_len=seq_len, stop=stop)
    bld.mk.use_scopes = trace
    nc = bld.main()
    res = run_bass_kernel_spmd(nc, in_maps, core_ids=list(range(n_cores)), trace=trace)
    outs = [np.ascontiguousarray(r["out_fm"].transpose(0, 2, 1)) for r in res.results]
    return np.concatenate(outs, axis=0), res


def kernel(**inputs):
    inp = {k: np.asarray(v) for k, v in inputs.items()}
    out, _ = run_cores(inp, 8, 2, 4096)
    return out.astype(np.float32)
```

```python
import numpy as np
from concourse.bass_utils import run_bass_kernel_spmd
import bisect
from contextlib import ExitStack
import concourse.bass as bass
import concourse.mybir as mybir

F32 = mybir.dt.float32
BF16 = mybir.dt.bfloat16
I32 = mybir.dt.int32
AF = mybir.ActivationFunctionType
ALU = mybir.AluOpType
AX = mybir.AxisListType
ESZ = {F32: 4, BF16: 2, I32: 4}

COMPUTE = ("pe", "act", "dve", "pool")
ENGS = ("pe", "act", "dve", "pool", "sp")
DMA_POOL = 24


class V:
    def __init__(self, ap, space, base, fshape, dtype):
        self.ap = ap
        self.space = space
        self.base = base
        self.fshape = tuple(fshape)
        self.dtype = dtype
        es = ESZ[dtype]
        st = []
        s = es
        for d in reversed(self.fshape):
            st.append(s)
            s *= d
        self.fstrides = tuple(reversed(st))
        self.lo = base
        self.hi = base + s
        self.full_iv = None

    def __getitem__(self, key):
        if not isinstance(key, tuple):
            key = (key,)
        key = key + (slice(None),) * (1 + len(self.fshape) - len(key))
        ap = self.ap[key]
        lo = self.base
        hi = self.base
        newshape = []
        for k, d, st in zip(key[1:], self.fshape, self.fstrides):
            if isinstance(k, int):
                a, b = k, k + 1
            else:
                a = 0 if k.start is None else k.start
                b = d if k.stop is None else k.stop
                assert k.step in (None, 1)
                newshape.append(b - a)
            assert 0 <= a < b <= d, (key, self.fshape)
            lo += a * st
            hi += (b - 1) * st
        v = V.__new__(V)
        v.ap = ap
        v.space = self.space
        v.dtype = self.dtype
        v.base = lo
        v.fshape = tuple(newshape)
        v.fstrides = tuple(st for k, st in zip(key[1:], self.fstrides) if not isinstance(k, int))
        v.lo = lo
        v.hi = hi + ESZ[self.dtype]
        v.full_iv = self.full_iv
        if self.full_iv is not None:
            v.lo, v.hi = self.full_iv
        return v


class Key:
    def __init__(self, space, lo=0, hi=1):
        self.space, self.lo, self.hi = space, lo, hi


class Op:
    __slots__ = ("eng", "idx", "fn", "deps", "sig", "dma_sem", "dma_target", "ndma", "has_dep", "name", "label", "gidx", "cost", "nbytes", "fin", "nleft", "users", "ridy")


class MK:
    def __init__(self, nc):
        self.nc = nc
        self.streams = {e: [] for e in ENGS}
        self.segs = {}
        self.dma_count = {e: 0 for e in ENGS}
        self.dma_ops = {e: [] for e in ENGS}
        self.nsem_dma = {}
        self.label = None
        self.use_scopes = False
        self.gcount = 0

    def _segments(self, space, lo, hi):
        L = self.segs.setdefault(space, [])
        i = bisect.bisect_right(L, lo, key=lambda s: s[0]) - 1
        if i >= 0 and L[i][1] > lo:
            s = L[i]
            if s[0] < lo:
                L.insert(i + 1, [lo, s[1], s[2], list(s[3]), list(s[4])])
                s[1] = lo
                i += 1
        else:
            i += 1
        out = []
        cur = lo
        j = i
        while cur < hi:
            if j < len(L) and L[j][0] == cur:
                s = L[j]
                if s[1] > hi:
                    L.insert(j + 1, [hi, s[1], s[2], list(s[3]), list(s[4])])
                    s[1] = hi
                out.append(s)
                cur = s[1]
                j += 1
            else:
                nxt = min(hi, L[j][0]) if j < len(L) else hi
                g = [cur, nxt, None, [], []]
                L.insert(j, g)
                out.append(g)
                cur = nxt
                j += 1
        return out

    def _merge(self, space):
        L = self.segs[space]
        if len(L) < 256:
            return
        M = [L[0]]
        for s in L[1:]:
            p = M[-1]
            if p[1] == s[0] and p[2] is s[2] and p[3] == s[3] and p[4] == s[4]:
                p[1] = s[1]
            else:
                M.append(s)
        self.segs[space] = M

    def op(self, eng, fn, reads=(), writes=(), ndma=0, name=None, cost=None, nbytes=0):
        o = Op()
        o.gidx = self.gcount
        self.gcount += 1
        if cost is None:
            n = 64
            if writes and hasattr(writes[0], "fshape"):
                n = 1
                for d_ in writes[0].fshape:
                    n *= d_
            if eng == "pe":
                cost = 250.0
            elif eng == "act":
                cost = 220.0 + n / 1.2
            elif eng == "dve":
                cost = 120.0 + n / 0.96
            else:
                cost = 250.0 + n / 0.7
        o.cost = cost
        o.nbytes = nbytes
        o.eng = eng
        o.fn = fn
        o.idx = len(self.streams[eng])
        o.ndma = ndma
        o.sig = None
        o.has_dep = False
        o.name = name
        o.dma_sem = None
        o.label = self.label
        deps = {}
        is_dma = ndma > 0

        def add(d):
            if d is not None and d is not o:
                deps[id(d)] = d

        for r in reads:
            for s in self._segments(r.space, r.lo, r.hi):
                add(s[2])
                if is_dma:
                    s[4].append(o)
                else:
                    s[3].append(o)
        for w in writes:
            for s in self._segments(w.space, w.lo, w.hi):
                add(s[2])
                for rd in s[3]:
                    add(rd)
                for rd in s[4]:
                    add(rd)
                s[2] = o
                s[3] = []
                s[4] = []
            self._merge(w.space)
        if is_dma:
            k = self.dma_count[eng]
            self.dma_count[eng] += 1
            o.dma_sem = k % DMA_POOL
            lst = self.dma_ops[eng]
            if k >= DMA_POOL:
                add(lst[k - DMA_POOL])
            lst.append(o)
        o.deps = list(deps.values())
        for d in o.deps:
            d.has_dep = True
        self.streams[eng].append(o)
        return o

    def schedule(self, window=64, lat=120.0):
        allops = []
        for e in ENGS:
            allops.extend(self.streams[e])
        for o in allops:
            o.users = []
            o.fin = None
        for o in allops:
            o.nleft = len(o.deps)
            o.ridy = 0.0
            for d in o.deps:
                d.users.append(o)
        orig = {e: list(self.streams[e]) for e in ENGS}
        head = {e: 0 for e in ENGS}
        done = set()
        ready = {e: [] for e in ENGS}
        inwin = {e: 0 for e in ENGS}
        waiting = {}
        free_at = {e: 0.0 for e in ENGS}
        new = {e: [] for e in ENGS}
        dma_bw_free = [0.0]
        total = len(allops)
        nsched = 0

        def admit(e):
            lim = min(len(orig[e]), head[e] + window)
            while inwin[e] < lim:
                o = orig[e][inwin[e]]
                inwin[e] += 1
                if id(o) not in done and o.nleft == 0:
                    ready[e].append(o)
        for e in ENGS:
            admit(e)
        while nsched < total:
            best = None
            for e in ENGS:
                fa = free_at[e]
                for o in ready[e]:
                    st = o.ridy if o.ridy > fa else fa
                    k = (st, o.gidx)
                    if best is None or k < best[0]:
                        best = (k, o)
            assert best is not None, "scheduler stuck"
            (st, _), o = best
            e = o.eng
            ready[e].remove(o)
            if o.ndma > 0:
                issue = 1200.0 if e == "pool" else 120.0
                t0 = max(st + issue, dma_bw_free[0])
                dur = o.nbytes / 200.0
                dma_bw_free[0] = t0 + dur
                o.fin = t0 + dur + 1800.0
                free_at[e] = st + issue
            else:
                o.fin = st + o.cost
                free_at[e] = o.fin
            done.add(id(o))
            new[e].append(o)
            nsched += 1
            lst = orig[e]
            h = head[e]
            while h < len(lst) and id(lst[h]) in done:
                h += 1
            head[e] = h
            admit(e)
            for u in o.users:
                u.nleft -= 1
                f = o.fin + lat
                if f > u.ridy:
                    u.ridy = f
                if u.nleft == 0:
                    ue = u.eng
                    if u.idx < inwin[ue]:
                        ready[ue].append(u)
        for e in ENGS:
            self.streams[e] = new[e]
        self.makespan = max(free_at.values())
        for e in ENGS:
            k = 0
            lst = []
            for o in self.streams[e]:
                if o.ndma > 0:
                    o.dma_sem = k % DMA_POOL
                    if k >= DMA_POOL:
                        p = lst[k - DMA_POOL]
                        if p not in o.deps:
                            o.deps.append(p)
                            p.has_dep = True
                    lst.append(o)
                    k += 1
            self.dma_ops[e] = lst

    def emit(self, final_wait_ops=()):
        nc = self.nc
        with ExitStack() as es:
            CH = 16000
            nsigs = {e: sum(1 for o in self.streams[e] if o.ndma == 0 and o.has_dep) for e in COMPUTE}
            csem = {e: [es.enter_context(nc.semaphore(f"c_{e}_{i}")) for i in range(nsigs[e] // CH + 1)] for e in COMPUTE}
            dsem = {}
            for e in ENGS:
                if self.dma_count[e] > 0:
                    for i in range(min(DMA_POOL, self.dma_count[e])):
                        dsem[(e, i)] = es.enter_context(nc.semaphore(f"d_{e}_{i}"))
            for e in COMPUTE:
                c = 0
                for o in self.streams[e]:
                    if o.ndma == 0 and o.has_dep:
                        c += 1
                        o.sig = c
            for e in ENGS:
                tot = {}
                for o in self.dma_ops[e]:
                    t = tot.get(o.dma_sem, 0) + 16 * o.ndma
                    tot[o.dma_sem] = t
                    o.dma_target = t
            self.nsig = {e: sum(1 for o in self.streams[e] if o.sig) for e in COMPUTE}
            block = es.enter_context(nc.Block())
            mk = self

            def run(eng_name, E):
                known = {}
                nwait = 0
                cur = None
                cm = None
                for o in mk.streams[eng_name]:
                    if mk.use_scopes and o.label != cur:
                        if cm is not None:
                            cm.__exit__(None, None, None)
                            cm = None
                        cur = o.label
                        if cur is not None:
                            cm = nc.named_scope(cur)
                            cm.__enter__()
                    need = {}
                    for d in o.deps:
                        if d.ndma > 0:
                            key = ("d", d.eng, d.dma_sem)
                            val = d.dma_target
                        else:
                            if d.eng == "pe" and eng_name == "pe":
                                continue
                            key = ("c", d.eng)
                            val = d.sig
                        if known.get(key, 0) >= val:
                            continue
                        if need.get(key, 0) < val:
                            need[key] = val
                    for key, val in need.items():
                        if key[0] == "c":
                            E.wait_ge(csem[key[1]][(val - 1) // CH], (val - 1) % CH + 1)
                        else:
                            E.wait_ge(dsem[(key[1], key[2])], val)
                        known[key] = val
                        nwait += 1
                    r = o.fn(E)
                    if o.ndma > 0:
                        ins = r if isinstance(r, (list, tuple)) else [r]
                        assert len(ins) == o.ndma, (len(ins), o.ndma, o.name)
                        for i_ in ins:
                            i_.then_inc(dsem[(eng_name, o.dma_sem)], 16)
                    elif o.sig is not None:
                        r.then_inc(csem[eng_name][(o.sig - 1) // CH], 1)
                if cm is not None:
                    cm.__exit__(None, None, None)
                mk.nwait = getattr(mk, "nwait", 0) + nwait

            @block.tensor
            def _(E):
                run("pe", E)

            @block.scalar
            def _(E):
                run("act", E)

            @block.vector
            def _(E):
                run("dve", E)

            @block.gpsimd
            def _(E):
                run("pool", E)

            @block.sync
            def _(E):
                run("sp", E)

D = 1024
DFF = 2816
NF = DFF // 128
T = 512
NB = T // 128
EPS = 1e-6
SCALE = 96 ** -0.5

CP_NORM = 0
CP_CONVW = 48
CP_CONVB = 176
CP_N = 208
RP_DTB = 0
RP_ALOG = 32
RP_D = 64
RP_SSMNW = 96
RP_QA = 2144
RP_KVA = 2528
RP_QN = 2784
RP_KN = 2880
RP_INVF = 2976
RP_N = 2992
CK_ID = 0
CK_TRIU = 128
CK_USTR = 256
CK_SEL = 384
CK_N = 416


class Builder:
    def __init__(self, n_seq=2, seq_len=4096, stop=None, do_prepass=True):
        self.n_seq = n_seq
        self.seq_len = seq_len
        self.stop = stop
        self.nt = seq_len // T
        nc = bass.Bass("TRN2", target_bir_lowering=False)
        self.nc = nc
        self.mk = MK(nc)
        self.es = ExitStack()

    def dram_in(self, name, shape, dt=F32):
        return self.nc.dram_tensor(name, list(shape), dt, kind="ExternalInput").ap()

    def dram_scr(self, name, shape, dt=BF16):
        return self.nc.dram_tensor(name, list(shape), dt, kind="Internal").ap()

    def alloc(self, fshape, dt, name=None):
        n = int(np.prod(fshape)) * ESZ[dt]
        n4 = (n + 3) // 4
        off = self.aoff
        self.aoff += (n4 + 7) // 8 * 8
        assert self.aoff <= self.acap, ("arena overflow", name, self.aoff, self.acap)
        return self.view(off, fshape, dt)

    def view(self, off_f32, fshape, dt):
        n = int(np.prod(fshape)) * ESZ[dt]
        n4 = (n + 3) // 4
        ap = self.arena[:, off_f32:off_f32 + n4]
        if dt != F32:
            ap = ap.bitcast(dt)
            ap = ap[:, 0:int(np.prod(fshape))]
        if len(fshape) == 2:
            ap = ap.rearrange("p (a b) -> p a b", a=fshape[0])
        elif len(fshape) == 3:
            ap = ap.rearrange("p (a b c) -> p a b c", a=fshape[0], b=fshape[1])
        return V(ap, "sb", off_f32 * 4, fshape, dt)

    def ring_alloc(self, fshape, dt):
        n = int(np.prod(fshape)) * ESZ[dt]
        n4 = ((n + 3) // 4 + 7) // 8 * 8
        if self.rpos + n4 > self.rcap:
            self.rpos = 0
        off = self.rbase + self.rpos
        self.rpos += n4
        return self.view(off, fshape, dt)

    def psum(self, bank, fshape, dt=F32, off=0):
        t = self.pbanks[bank]
        n = int(np.prod(fshape))
        if dt == F32:
            ap = t[:, off // 4: off // 4 + n]
        else:
            ap = t[:].bitcast(BF16)[:, off // 2: off // 2 + n]
        if len(fshape) == 2:
            ap = ap.rearrange("p (a b) -> p a b", a=fshape[0])
        elif len(fshape) == 3:
            ap = ap.rearrange("p (a b c) -> p a b c", a=fshape[0], b=fshape[1])
        v = V(ap, f"ps{bank}", off, fshape, dt)
        v.full_iv = (0, 2048)
        v.lo, v.hi = 0, 2048
        return v

    def dma(self, eng, out, in_, reads, writes, name=None):
        nb = ESZ.get(out.dtype, 4)
        for d_ in out.shape:
            nb *= d_
        self.mk.op(eng, lambda E: E.dma_start(out=out, in_=in_), reads=reads, writes=writes, ndma=1, name=name,
                   nbytes=nb)

    def mm_group(self, out, pairs, reads, name=None):
        oap = out.ap
        n = len(pairs)

        def fn(E):
            r = None
            for i, (l, rr) in enumerate(pairs):
                r = E.matmul(oap, l, rr, start=(i == 0), stop=(i == n - 1))
            return r
        nn = 1
        for d_ in out.fshape:
            nn *= d_
        self.mk.op("pe", fn, reads=reads, writes=[out], name=name, cost=n * (max(nn, 96) / 2.4 + 8.0))

    def transpose(self, out, in_, ident, name=None):
        self.mk.op("pe", lambda E: E.transpose(out.ap, in_.ap, ident.ap), reads=[in_, ident], writes=[out], name=name,
                   cost=70.0)

    def act(self, out, in_, func, bias=None, scale=None, accum=None, extra_reads=(), eng="act"):
        kw = {}
        reads = [in_] + list(extra_reads)
        writes = [out]
        if bias is not None:
            if isinstance(bias, V):
                kw["bias"] = bias.ap
                reads.append(bias)
            else:
                kw["bias"] = bias
        if scale is not None:
            if isinstance(scale, V):
                kw["scale"] = scale.ap
                reads.append(scale)
            else:
                kw["scale"] = scale
        if accum is not None:
            kw["accum_out"] = accum.ap
            writes.append(accum)
        oa, ia = out.ap, in_.ap
        self.mk.op("act", lambda E: E.activation(oa, ia, func, **kw), reads=reads, writes=writes)

    def tt(self, eng, out, a, b, op, a_ap=None, b_ap=None):
        aa = a.ap if a_ap is None else a_ap
        ba = b.ap if b_ap is None else b_ap
        oa = out.ap
        self.mk.op(eng, lambda E: E.tensor_tensor(oa, aa, ba, op), reads=[a, b], writes=[out])

    def ts(self, eng, out, a, s1, s2, op0, op1=None, a_ap=None):
        aa = a.ap if a_ap is None else a_ap
        oa = out.ap
        reads = [a]
        v1 = s1
        v2 = s2
        if isinstance(s1, V):
            reads.append(s1)
            v1 = s1.ap
        if isinstance(s2, V):
            reads.append(s2)
            v2 = s2.ap
        if op1 is None:
            self.mk.op(eng, lambda E: E.tensor_scalar(oa, aa, v1, None, op0), reads=reads, writes=[out])
        else:
            self.mk.op(eng, lambda E: E.tensor_scalar(oa, aa, v1, v2, op0, op1), reads=reads, writes=[out])

    def stt(self, eng, out, a, s, b, op0, op1, a_ap=None, b_ap=None):
        aa = a.ap if a_ap is None else a_ap
        ba = b.ap if b_ap is None else b_ap
        oa = out.ap
        reads = [a, b]
        sv = s
        if isinstance(s, V):
            reads.append(s)
            sv = s.ap
        self.mk.op(eng, lambda E: E.scalar_tensor_tensor(oa, aa, sv, ba, op0, op1), reads=reads, writes=[out])

    def copy(self, eng, out, in_, in_ap=None):
        ia = in_.ap if in_ap is None else in_ap
        oa = out.ap
        if eng == "act":
            self.mk.op("act", lambda E: E.copy(oa, ia), reads=[in_], writes=[out])
        else:
            self.mk.op(eng, lambda E: E.tensor_copy(oa, ia), reads=[in_], writes=[out])

    def memset(self, eng, out, val):
        oa = out.ap
        self.mk.op(eng, lambda E: E.memset(oa, val), reads=[], writes=[out])
import os as _os
def _b_setup(self):
    nc = self.nc
    S = self.seq_len
    ns = self.n_seq
    es = self.es
    self.x_fm = self.dram_in("x_fm", [ns, D, S])
    self.pos_l = self.dram_in("pos_l", [ns, 128, S // 128], I32)
    self.colpack_d = self.dram_in("colpack", [128, CP_N])
    self.rowpack_d = self.dram_in("rowpack", [128, RP_N])
    self.constpack_d = self.dram_in("constpack", [128, CK_N])
    self.out_fm = nc.dram_tensor("out_fm", [ns, D, S], F32, kind="ExternalOutput").ap()
    wspecs = {
        "ffn_g": [4 * NF * 128, 1024], "ffn_u": [4 * NF * 128, 1024], "ffn_d": [4 * NF * 2 * 128, 512],
        "ssm_win_z": [4 * 128, 8 * 512], "ssm_win_x": [32 * 128, 8 * 128], "ssm_win_dt": [128, 8 * 32],
        "ssm_wout": [8 * 128, 16 * 128],
        "mla_win": [128, 8 * 672], "mla_wqb": [128, 3 * 1536], "mla_wkvb": [128, 2 * 2048],
        "mla_wout": [8 * 128, 8 * 128],
    }
    self.w32 = {}
    self.w16 = {}
    for k, shp in wspecs.items():
        self.w32[k] = self.dram_in(k, shp)
        self.w16[k] = self.dram_scr(k + "_bf", shp)
    self.kc_d = self.dram_scr("kcache", [16, 96, S])
    self.vc_d = self.dram_scr("vcache", [16, 128, (S // 128) * 64])
    self.acap = 51680
    self.arena = es.enter_context(nc.sbuf_tensor("arena", [128, self.acap], F32))
    self.aoff = 0
    self.pbanks = [es.enter_context(nc.psum_tensor(f"psb{i}", [128, 512], F32)) for i in range(8)]
    self.colpack = self.alloc([CP_N], F32)
    self.rowpack = self.alloc([RP_N], F32)
    self.ck = self.alloc([CK_N], F32)
    self.ident_bf = self.alloc([128], BF16)
    self.triu_bf = self.alloc([128], BF16)
    self.ustr_bf = self.alloc([128], BF16)
    self.ones_bf = self.alloc([128], BF16)
    self.sel_bf = self.alloc([32], BF16)
    self.selneg_bf = self.alloc([32], BF16)
    self.mneg_bf = self.alloc([128], BF16)
    self.ones_d = self.alloc([128], BF16)
    self.a_row = self.alloc([32], F32)
    self.qw_row = self.alloc([96], F32)
    self.invn3 = self.alloc([3], F32)
    self.X = [self.alloc([8, T], F32), self.alloc([8, T], F32)]
    self.S_state = self.alloc([2048], F32)
    self.S_bf = self.alloc([2048], BF16)
    self.carry = self.alloc([32, 3], F32)
    self.posf = self.alloc([S // 128], F32)
    self.posi = self.alloc([S // 128], I32)
    self.rcap = 8192
    self.rbase = self.aoff
    self.aoff += self.rcap
    self.rpos = 0
    mk = self.mk
    self.dma("sp", self.colpack.ap, self.colpack_d, [], [self.colpack])
    self.dma("sp", self.rowpack.ap, self.rowpack_d, [], [self.rowpack])
    self.dma("sp", self.ck.ap, self.constpack_d, [], [self.ck])
    self.copy("dve", self.ident_bf, self.ck[:, CK_ID:CK_ID + 128])
    self.copy("dve", self.triu_bf, self.ck[:, CK_TRIU:CK_TRIU + 128])
    self.copy("dve", self.ustr_bf, self.ck[:, CK_USTR:CK_USTR + 128])
    self.memset("pool", self.ones_bf, 1.0)
    self.copy("dve", self.sel_bf, self.ck[:, CK_SEL:CK_SEL + 32])
    self.ts("dve", self.selneg_bf, self.ck[:, CK_SEL:CK_SEL + 32], -1.0, None, ALU.mult)
    self.ts("dve", self.mneg_bf, self.ck[:, CK_USTR:CK_USTR + 128], -60000.0, None, ALU.mult)
    self.memset("pool", self.ones_d, 1.0 / 1024)
    for i_, v_ in enumerate((1.0 / 384, 1.0 / 256, 1.0 / 32)):
        self.memset("pool", self.invn3[:, i_:i_ + 1], v_)
    self.act(self.a_row, self.rowpack[:, RP_ALOG:RP_ALOG + 32], AF.Exp)
    self.ts("dve", self.a_row, self.a_row, -1.0, None, ALU.mult)
    self.tt("dve", self.qw_row[:, 0:64], self.rowpack[:, RP_QN:RP_QN + 64], self.rowpack[:, RP_KN:RP_KN + 64], ALU.mult)
    self.copy("dve", self.qw_row[:, 64:96], self.rowpack[:, RP_QN + 64:RP_QN + 96])
    self.ident = self.ck[:, CK_ID:CK_ID + 128]
    self.triu = self.ck[:, CK_TRIU:CK_TRIU + 128]
    self.ustr = self.ck[:, CK_USTR:CK_USTR + 128]


def _b_prepass(self):
    def rows(k, fi=None):
        n = self.w32[k].shape[0]
        if fi is None:
            return (k, 0, n)
        per = n // 4
        return (k, fi * per, (fi + 1) * per)
    plan = [rows("ffn_g", 0), rows("ffn_u", 0), rows("ffn_d", 0),
            rows("ssm_win_z"), rows("ssm_win_dt"), rows("ssm_win_x"), rows("ssm_wout"),
            rows("ffn_g", 1), rows("ffn_u", 1), rows("ffn_d", 1),
            rows("ffn_g", 2), rows("ffn_u", 2), rows("ffn_d", 2),
            rows("mla_win"), rows("mla_wqb"), rows("mla_wkvb"), rows("mla_wout"),
            rows("ffn_g", 3), rows("ffn_u", 3), rows("ffn_d", 3)]
    for (k, a, b_) in plan:
        src = self.w32[k]
        dst = self.w16[k]
        cols = src.shape[1]
        step = max(128, (262144 // cols) // 128 * 128)
        for r0 in range(a, b_, step):
            r1 = min(b_, r0 + step)
            self.dma("pool", dst[r0:r1, :], src[r0:r1, :], [], [Key("w_" + k, r0, r1)], name="cvt")


def _wload(self, key, r0, fshape, eng="sp"):
    v = self.ring_alloc(fshape, BF16)
    src = self.w16[key][r0:r0 + 128, :]
    dst = v.ap
    if len(fshape) == 2:
        src = src.rearrange("p (a b) -> p a b", a=fshape[0])
    self.dma(eng, dst, src, [Key("w_" + key, r0, r0 + 128)], [v], name="wld")
    return v


def _b_rmsnorm_fm(self, X, ni, hT):
    mark = self.aoff
    sq = self.alloc([8, T], BF16)
    rstd = self.alloc([T], F32)
    for kc in range(8):
        self.act(sq[:, kc, :], X[:, kc, :], AF.Square)
    ps = self.psum(7, [T])
    self.mm_group(ps, [(self.ones_d.ap, sq[:, kc, :].ap) for kc in range(8)], reads=[self.ones_d, sq])
    self.rsqrt_eps(rstd, ps)
    for kc in range(8):
        c = CP_NORM + ni * 8 + kc
        self.stt("dve", hT[:, kc, :], X[:, kc, :], self.colpack[:, c:c + 1], rstd,
                 ALU.mult, ALU.mult)
    self.aoff = mark


def _b_rsqrt_eps(self, out, in_):
    self.ts("dve", out, in_, EPS, None, ALU.add)
    self.act(out, out, AF.Ln)
    self.act(out, out, AF.Exp, scale=-0.5)


def _b_ffn(self, X, fi, ni, slot=0):
    mark = self.aoff
    if slot:
        self.aoff += 13568
    hT = self.alloc([8, T], BF16)
    gT = self.alloc([NF, T], BF16)
    sg = [self.alloc([T], F32) for _ in range(3)]
    su = [self.alloc([T], F32) for _ in range(3)]
    rstd = self.alloc([T], F32)
    sq = self.alloc([8, T], BF16)
    for kc in range(8):
        c = CP_NORM + ni * 8 + kc
        self.act(hT[:, kc, :], X[:, kc, :], AF.Copy, scale=self.colpack[:, c:c + 1])
    for kc in range(8):
        self.act(sq[:, kc, :], X[:, kc, :], AF.Square)
    ps7 = self.psum(7, [T])
    self.mm_group(ps7, [(self.ones_d.ap, sq[:, kc, :].ap) for kc in range(8)], reads=[self.ones_d, sq])
    self.rsqrt_eps(rstd, ps7)
    for f in range(NF):
        r0 = (fi * NF + f) * 128
        wg = self.wload("ffn_g", r0, [8, 128])
        wu = self.wload("ffn_u", r0, [8, 128])
        pg = self.psum(f % 2, [T])
        pu = self.psum(2 + f % 2, [T])
        self.mm_group(pg, [(wg[:, kc, :].ap, hT[:, kc, :].ap) for kc in range(8)], reads=[wg, hT])
        self.mm_group(pu, [(wu[:, kc, :].ap, hT[:, kc, :].ap) for kc in range(8)], reads=[wu, hT])
        s = sg[f % 3]
        u2 = su[f % 3]
        self.tt("dve", s, pg, rstd, ALU.mult)
        self.tt("dve", u2, pu, rstd, ALU.mult)
        self.act(s, s, AF.Silu)
        self.tt("pool", gT[:, f, :], s, u2, ALU.mult)
    for half in range(2):
        pss = [self.psum((4 if half == 0 else 0) + c, [T]) for c in range(4)]
        for f in range(NF):
            r0 = ((fi * NF + f) * 2 + half) * 128
            wd = self.wload("ffn_d", r0, [512])
            g_ap = gT[:, f, :].ap
            w_aps = [wd[:, c * 128:(c + 1) * 128].ap for c in range(4)]
            p_aps = [p.ap for p in pss]

            def fn(E, w_aps=w_aps, g_ap=g_ap, p_aps=p_aps, f=f):
                r = None
                for c in range(4):
                    r = E.matmul(p_aps[c], w_aps[c], g_ap, start=(f == 0), stop=(f == NF - 1))
                return r
            self.mk.op("pe", fn, reads=[wd, gT[:, f, :]], writes=pss, cost=4 * (512 / 2.4 + 8))
        for c in range(4):
            dc = half * 4 + c
            self.stt("dve", X[:, dc, :], pss[c], 0.5, X[:, dc, :], ALU.mult, ALU.add)
    self.aoff = mark


def _b_main(self):
    self.setup()
    self.prepass()
    stop = self.stop
    tiles = [(b, t) for b in range(self.n_seq) for t in range(self.nt)]

    def xload(i):
        b_, t_ = tiles[i]
        src = self.x_fm[b_, :, t_ * T:(t_ + 1) * T].rearrange("(kc p) t -> p kc t", p=128)
        self.dma("sp", self.X[i % 2].ap, src, [], [self.X[i % 2]], name="xload")
    xload(0)
    for ti, (b, t) in enumerate(tiles):
        if t == 0:
            self.seq_begin(b)
        X = self.X[ti % 2]
        t0 = t * T
        phases = [
            lambda: self.ffn(X, 0, 0, slot=1),
            lambda: self.mamba(X, b, t),
            lambda: self.ffn(X, 1, 2),
            lambda: self.ffn(X, 2, 3),
            lambda: self.mla(X, b, t),
            lambda: self.ffn(X, 3, 5),
        ]
        for pi, ph in enumerate(phases):
            if stop is not None and pi >= stop:
                break
            self.mk.label = "p%d_t%d" % (pi, ti)
            ph()
            if pi == 3 and ti + 1 < len(tiles):
                self.mk.label = None
                xload(ti + 1)
        self.mk.label = None
        if stop is not None and stop <= 3 and ti + 1 < len(tiles):
            xload(ti + 1)
        dst = self.out_fm[b, :, t0:t0 + T].rearrange("(kc p) t -> p kc t", p=128)
        self.dma("sp", dst, X.ap, [X], [Key("out", ti, ti + 1)], name="xstore")
    ti = len(tiles)
    self.mk.op("sp", lambda E: None, reads=[Key("out", 0, ti)], writes=[])
    if _os.environ.get("KSCHED", "1") != "0":
        self.mk.schedule(window=int(_os.environ.get("KWIN", "64")), lat=float(_os.environ.get("KLAT", "120")))
    self.mk.emit()
    self.es.close()
    return self.nc


def _b_seq_begin(self, b):
    self.memset("pool", self.S_state, 0.0)
    self.memset("pool", self.S_bf, 0.0)
    self.memset("pool", self.carry, 0.0)
    self.dma("sp", self.posi.ap, self.pos_l[b], [], [self.posi])
    self.copy("dve", self.posf, self.posi)


Builder.setup = _b_setup
Builder.prepass = _b_prepass
Builder.wload = _wload
Builder.rmsnorm_fm = _b_rmsnorm_fm
Builder.ffn = _b_ffn
Builder.rsqrt_eps = _b_rsqrt_eps
Builder.main = _b_main
Builder.seq_begin = _b_seq_begin
def _bc(v, dims):
    a = v.ap
    return bass.AP(tensor=a.tensor, offset=a.offset, ap=[list(a.ap[0])] + [list(d) for d in dims])


_DBG = int(_os.environ.get('KDBG', '0'))


def _b_mamba(self, X, b, t):
    mark = self.aoff
    mk = self.mk
    zs = self.alloc([NB, 2048], BF16)
    x_tm = self.alloc([NB, 2048], BF16)
    B_tm = self.alloc([NB, 1024], BF16)
    BC = self.alloc([16, T], BF16)
    dt = self.alloc([NB, 32], F32)
    dtA3 = self.alloc([NB, 3, 32], BF16)
    gT = self.alloc([16, T], BF16)
    mark2 = self.aoff
    hT = self.alloc([8, T], BF16)
    self.rmsnorm_fm(X, 1, hT)
    rp = self.rowpack
    cp = self.colpack
    for cg in range(4):
        w = self.wload("ssm_win_z", cg * 128, [8, 512])
        for blk in range(NB):
            ps = self.psum(blk % 2, [512])
            self.mm_group(ps, [(hT[:, kc, blk * 128:(blk + 1) * 128].ap, w[:, kc, :].ap) for kc in range(8)],
                          reads=[w, hT])
            self.act(zs[:, blk, cg * 512:(cg + 1) * 512], ps, AF.Silu)
    if _DBG == 1:
        self.aoff = mark
        return
    wdt = self.wload("ssm_win_dt", 0, [8, 32])
    tmp = self.alloc([6, 32], F32)
    for blk in range(NB):
        ps = self.psum(2, [32], off=blk * 128)
        self.mm_group(ps, [(hT[:, kc, blk * 128:(blk + 1) * 128].ap, wdt[:, kc, :].ap) for kc in range(8)],
                      reads=[wdt, hT])
        xb, nx, ax, e, l, r = (tmp[:, i, :] for i in range(6))
        self.tt("dve", xb, ps, rp[:, RP_DTB:RP_DTB + 32], ALU.add)
        self.ts("dve", nx, xb, -1.0, None, ALU.mult)
        self.tt("dve", ax, xb, nx, ALU.max)
        self.act(e, ax, AF.Exp, scale=-1.0)
        self.ts("dve", e, e, 1.0, None, ALU.add)
        self.act(l, e, AF.Ln)
        self.ts("dve", r, xb, 0.0, None, ALU.max)
        d = dt[:, blk, :]
        self.tt("dve", d, r, l, ALU.add)
        dA, r1, r2 = tmp[:, 0, :], tmp[:, 1, :], tmp[:, 2, :]
        self.tt("dve", dA, d, self.a_row, ALU.mult)
        p1, p2, p3 = (dtA3[:, blk, j, :] for j in range(3))
        self.copy("dve", p1, dA)
        self.tt("dve", r1, dA, p1, ALU.subtract)
        self.copy("dve", p2, r1)
        self.tt("dve", r2, r1, p2, ALU.subtract)
        self.copy("dve", p3, r2)
    if _DBG == 2:
        self.aoff = mark
        return
    pre = [self.alloc([T + 3], F32), self.alloc([T + 3], F32)]
    acc = [self.alloc([T], F32), self.alloc([T], F32)]
    xst = [self.alloc([T], BF16), self.alloc([T], BF16)]
    for cc in range(32):
        w = self.wload("ssm_win_x", cc * 128, [8, 128])
        ps = self.psum(cc % 2, [T])
        self.mm_group(ps, [(w[:, kc, :].ap, hT[:, kc, :].ap) for kc in range(8)], reads=[w, hT])
        p = pre[cc % 2]
        a = acc[cc % 2]
        self.copy("pool", p[:, 0:3], self.carry[:, cc, :])
        self.copy("act", p[:, 3:T + 3], ps)
        cw = CP_CONVW + cc * 4
        self.ts("dve", a, p[:, 0:T], cp[:, cw:cw + 1], cp[:, CP_CONVB + cc:CP_CONVB + cc + 1], ALU.mult, ALU.add)
        for k in range(1, 4):
            self.stt("dve", a, p[:, k:k + T], cp[:, cw + k:cw + k + 1], a, ALU.mult, ALU.add)
        self.copy("pool", self.carry[:, cc, :], p[:, T:T + 3])
        if cc < 16:
            dst = xst[cc % 2]
        else:
            dst = BC[:, cc - 16, :]
        self.act(dst, a, AF.Silu)
        if cc < 24:
            pt = self.psum(2 + cc % 2, [NB, 128], BF16)
            for blk in range(NB):
                self.transpose(pt[:, blk, :], dst[:, blk * 128:(blk + 1) * 128], self.ident_bf)
            if cc < 16:
                self.copy("act", x_tm[:, :, cc * 128:(cc + 1) * 128], pt)
            else:
                self.copy("act", B_tm[:, :, (cc - 16) * 128:(cc - 15) * 128], pt)
    if _DBG == 3:
        self.aoff = mark
        return
    self.mk.label = (self.mk.label or "") + "_ssd"
    self.aoff = mark2
    sm = self.alloc([8, 32], F32)
    xdt_l = [self.alloc([2048], BF16)] * 2
    xw = [self.alloc([256], BF16) for _ in range(2)]
    CBm = [self.alloc([4, 128], F32), self.alloc([4, 128], F32)]
    A3 = self.alloc([3, 32], BF16)
    acsT = self.alloc([128], BF16)
    a3t = self.alloc([2, 32], F32)
    Eb = [self.alloc([4, 128], F32) for _ in range(2)]
    MT = [self.alloc([4, 128], BF16) for _ in range(3)]
    Y = self.alloc([2048], F32)
    tq = [self.alloc([256], F32) for _ in range(3)]
    uq = [self.alloc([256], F32) for _ in range(2)]
    junk = self.alloc([256], F32)
    ss = self.alloc([8], F32)
    gn = self.alloc([2048], BF16)
    for blk in range(NB):
        acs, eacs, lastb, dlast, dte = (sm[:, i, :] for i in range(5))
        xdt = xdt_l[blk % 2]
        parts = [dtA3[:, blk, j, :] for j in range(3)]
        ps_a = self.psum(7, [32], off=0)
        self.mm_group(ps_a, [(self.triu_bf.ap, pj.ap) for pj in parts], reads=[self.triu_bf] + parts)
        self.copy("dve", acs, ps_a)
        self.act(eacs, ps_a, AF.Exp)
        q1, q2, q3 = A3[:, 0, :], A3[:, 1, :], A3[:, 2, :]
        e1, e2 = a3t[:, 0, :], a3t[:, 1, :]
        self.copy("dve", q1, acs)
        self.tt("dve", e1, acs, q1, ALU.subtract)
        self.copy("dve", q2, e1)
        self.tt("dve", e2, e1, q2, ALU.subtract)
        self.copy("dve", q3, e2)
        pta = self.psum(6, [4, 128], BF16)
        self.transpose(pta[0:96, 0, :], self._flat(A3, 96), self.ident_bf)
        self.copy("act", acsT[0:96], pta[0:96, 0, :])
        ps_l = self.psum(7, [32], off=256)
        self.mm_group(ps_l, [(self.ones_bf.ap, pj.ap) for pj in parts], reads=[self.ones_bf] + parts)
        self.act(dlast, ps_l, AF.Exp)
        self.tt("dve", dte, ps_l, acs, ALU.subtract)
        self.act(dte, dte, AF.Exp)
        d = dt[:, blk, :]
        xa = x_tm[:, blk, :]
        self.mk.op("pool", lambda E, o=xdt.ap.rearrange("p (h d) -> p h d", h=32),
                   i0=xa.ap.rearrange("p (h d) -> p h d", h=32), i1=_bc(d, [[1, 32], [0, 64]]):
                   E.tensor_tensor(o, i0, i1, ALU.mult), reads=[xa, d], writes=[xdt])
        for half in range(2):
            ps_cb = self.psum(half, [4, 128])
            for gg in range(4):
                g = half * 4 + gg
                self.mm_group(ps_cb[:, gg, :], [(BC[:, g, blk * 128:(blk + 1) * 128].ap,
                                                 BC[:, 8 + g, blk * 128:(blk + 1) * 128].ap)],
                              reads=[BC[:, g, blk * 128:(blk + 1) * 128], BC[:, 8 + g, blk * 128:(blk + 1) * 128]])
            self.mk.op("dve", lambda E, o=CBm[half].ap, i0=ps_cb.ap, i1=_bc(self.triu, [[0, 4], [1, 128]]):
                       E.tensor_tensor(o, i0, i1, ALU.mult), reads=[ps_cb, self.triu], writes=[CBm[half]])
        self.memset("pool", ss, 0.0)
        for g in range(8):
            k = g % 2
            k3 = g % 3
            ps_s = self.psum(2 + k, [4, 128])
            for hh in range(4):
                h = g * 4 + hh
                sel_l = _bc(self.sel_bf[0:96, h:h + 1], [[0, 128]])
                sel_r = _bc(self.selneg_bf[0:96, h:h + 1], [[0, 128]])
                self.mm_group(ps_s[:, hh, :], [(sel_l, acsT[0:96].ap), (acsT[0:96].ap, sel_r),
                                               (self.ident_bf.ap, self.mneg_bf.ap)],
                              reads=[acsT, self.sel_bf, self.selneg_bf, self.ident_bf, self.mneg_bf])
            self.act(Eb[k], ps_s, AF.Exp)
            cb = CBm[g // 4][:, g % 4, :]
            self.mk.op("dve", lambda E, o=MT[k3].ap, i0=Eb[k].ap, i1=_bc(cb, [[0, 4], [1, 128]]):
                       E.tensor_tensor(o, i0, i1, ALU.mult), reads=[Eb[k], cb], writes=[MT[k3]])
            ps_y = self.psum(4 + k, [512])
            for hh in range(4):
                h = g * 4 + hh
                self.mm_group(ps_y[:, hh * 64:(hh + 1) * 64], [(MT[k3][:, hh, :].ap, xdt[:, h * 64:(h + 1) * 64].ap)],
                              reads=[MT[k3], xdt[:, h * 64:(h + 1) * 64]])
            Cg = BC[:, 8 + g, blk * 128:(blk + 1) * 128]
            Sg = self.S_bf[:, g * 256:(g + 1) * 256]
            self.mm_group(ps_y[:, 256:512], [(Cg.ap, Sg.ap)], reads=[Cg, Sg])
            ea = eacs[:, g * 4:(g + 1) * 4]
            self.mk.op("dve", lambda E, o=tq[k3].ap.rearrange("p (h d) -> p h d", h=4),
                       i0=ps_y[:, 256:512].ap.rearrange("p (h d) -> p h d", h=4), i1=_bc(ea, [[1, 4], [0, 64]]):
                       E.tensor_tensor(o, i0, i1, ALU.mult), reads=[ps_y[:, 256:512], ea], writes=[tq[k3]])
            xg = x_tm[:, blk, g * 256:(g + 1) * 256]
            Dr = rp[:, RP_D + g * 4:RP_D + g * 4 + 4]
            self.mk.op("pool", lambda E, o=uq[k].ap.rearrange("p (h d) -> p h d", h=4),
                       i0=xg.ap.rearrange("p (h d) -> p h d", h=4), i1=_bc(Dr, [[1, 4], [0, 64]]):
                       E.tensor_tensor(o, i0, i1, ALU.mult), reads=[xg, Dr], writes=[uq[k]])
            Yg = Y[:, g * 256:(g + 1) * 256]
            self.tt("dve", Yg, ps_y[:, 0:256], tq[k3], ALU.add)
            self.tt("pool", Yg, Yg, uq[k], ALU.add)
            self.tt("pool", Yg, Yg, zs[:, blk, g * 256:(g + 1) * 256], ALU.mult)
            self.act(junk, Yg, AF.Square, accum=ss[:, g:g + 1])
            de = dte[:, g * 4:(g + 1) * 4]
            xdg = xdt[:, g * 256:(g + 1) * 256]
            self.mk.op("pool", lambda E, o=xw[k].ap.rearrange("p (h d) -> p h d", h=4),
                       i0=xdg.ap.rearrange("p (h d) -> p h d", h=4), i1=_bc(de, [[1, 4], [0, 64]]):
                       E.tensor_tensor(o, i0, i1, ALU.mult), reads=[xdg, de], writes=[xw[k]])
            ps_st = self.psum(6, [256], off=k * 1024)
            Bg = B_tm[:, blk, g * 128:(g + 1) * 128]
            self.mm_group(ps_st, [(Bg.ap, xw[k].ap)], reads=[Bg, xw[k]])
            Sf = self.S_state[:, g * 256:(g + 1) * 256]
            dl = dlast[:, g * 4:(g + 1) * 4]
            self.mk.op("dve", lambda E, o=Sf.ap.rearrange("p (h d) -> p h d", h=4),
                       i0=Sf.ap.rearrange("p (h d) -> p h d", h=4), i1=_bc(dl, [[1, 4], [0, 64]]):
                       E.tensor_tensor(o, i0, i1, ALU.mult), reads=[Sf, dl], writes=[Sf])
            self.tt("dve", Sf, ps_st, Sf, ALU.add)
            self.copy("pool", Sg, Sf)
        self.ts("dve", ss, ss, 1.0 / 256, EPS, ALU.mult, ALU.add)
        self.act(ss, ss, AF.Ln)
        self.act(ss, ss, AF.Exp, scale=-0.5)
        for g in range(8):
            self.stt("dve", gn[:, g * 256:(g + 1) * 256], Y[:, g * 256:(g + 1) * 256], ss[:, g:g + 1],
                     rp[:, RP_SSMNW + g * 256:RP_SSMNW + (g + 1) * 256], ALU.mult, ALU.mult)
        for q in range(4):
            pt = self.psum(q % 2, [4, 128], BF16)
            for c4 in range(4):
                cc = q * 4 + c4
                self.transpose(pt[:, c4, :], gn[:, cc * 128:(cc + 1) * 128], self.ident_bf)
            self.copy("act", gT[:, q * 4:(q + 1) * 4, blk * 128:(blk + 1) * 128], pt)
    for dc in range(8):
        w = self.wload("ssm_wout", dc * 128, [16, 128])
        ps = self.psum(2 + dc % 2, [T])
        self.mm_group(ps, [(w[:, cc, :].ap, gT[:, cc, :].ap) for cc in range(16)], reads=[w, gT])
        self.stt("dve", X[:, dc, :], ps, 1.0, X[:, dc, :], ALU.mult, ALU.add)
    self.aoff = mark


Builder.mamba = _b_mamba
TWO_PI_HI = 6.28125
TWO_PI_LO = 0.0019353071795864769
PI = 3.141592653589793


def _b_bt(self, eng, out, a, b_ap_v, dims, op, a3=None, o3=None):
    oa = out.ap if o3 is None else o3
    aa = a.ap if a3 is None else a3
    ba = _bc(b_ap_v, dims)
    self.mk.op(eng, lambda E: E.tensor_tensor(oa, aa, ba, op), reads=[a, b_ap_v], writes=[out])


def _b_sincos(self, blk_g, cosv, sinv, sm):
    ang, kf, r, m, r2 = (sm[:, i, :] for i in range(5))
    ki = self.sc_ki
    rp = self.rowpack
    self.ts("dve", ang, rp[:, RP_INVF:RP_INVF + 16], self.posf[:, blk_g:blk_g + 1], None, ALU.mult)
    self.ts("dve", kf, ang, 1.0 / (2 * PI), None, ALU.mult)
    self.copy("dve", ki, kf)
    self.copy("dve", kf, ki)
    self.stt("dve", r, kf, -TWO_PI_HI, ang, ALU.mult, ALU.add)
    self.stt("dve", r, kf, -TWO_PI_LO, r, ALU.mult, ALU.add)

    def fold(dst, src):
        self.ts("dve", m, src, PI, -2 * PI, ALU.is_gt, ALU.mult)
        self.tt("dve", dst, src, m, ALU.add)
        self.ts("dve", m, dst, -PI, 2 * PI, ALU.is_lt, ALU.mult)
        self.tt("dve", dst, dst, m, ALU.add)
    fold(r2, r)
    self.act(sinv, r2, AF.Sin)
    self.ts("dve", r, r2, PI / 2, None, ALU.add)
    fold(r2, r)
    self.act(cosv, r2, AF.Sin)


def _b_rope(self, eng, out1, out2, x1, x2, cosv, sinv, tmp4, nh):
    a, b, c, d = tmp4
    dims = [[0, nh], [1, 16]]
    for (dst, src, cs) in ((a, x1, cosv), (b, x2, sinv), (c, x2, cosv), (d, x1, sinv)):
        self.bt("dve", dst, src, cs, dims, ALU.mult)
    self.tt("dve", out1, a, b, ALU.subtract)
    self.tt("dve", out2, c, d, ALU.add)


def _b_mla(self, X, b, t):
    mark = self.aoff
    rp = self.rowpack
    t0 = t * T
    kb0 = t0 // 128
    hT = self.alloc([8, T], BF16)
    QT = self.alloc([16, T], BF16)
    KT = self.alloc([16, T], BF16)
    Vst = self.alloc([16, NB, 64], BF16)
    rstdc = self.alloc([NB], F32)
    mark2 = self.aoff
    for kc in range(8):
        c = CP_NORM + 4 * 8 + kc
        self.act(hT[:, kc, :], X[:, kc, :], AF.Copy, scale=self.colpack[:, c:c + 1])
    sqn = self.alloc([8, T], BF16)
    for kc in range(8):
        self.act(sqn[:, kc, :], X[:, kc, :], AF.Square)
    psc = self.psum(7, [NB])
    for blk in range(NB):
        self.mm_group(psc[:, blk:blk + 1], [(sqn[:, kc, blk * 128:(blk + 1) * 128].ap, self.ones_d[:, 0:1].ap)
                                            for kc in range(8)], reads=[sqn, self.ones_d])
    self.rsqrt_eps(rstdc, psc)
    self.aoff = mark2
    proj_l = [self.alloc([672], F32) for _ in range(2)]
    ss3 = self.alloc([3], F32)
    cqn = self.alloc([384], BF16)
    ckvn = self.alloc([256], BF16)
    krn = self.alloc([32], F32)
    cqT = self.alloc([3, 128], BF16)
    ckvT = self.alloc([2, 128], BF16)
    Q32_l = [self.alloc([16, 96], F32) for _ in range(2)]
    KV32_l = [self.alloc([16, 128], F32) for _ in range(2)]
    SQ = self.alloc([16, 96], F32)
    tn = self.alloc([16, 64], F32)
    u = self.alloc([16, 32], F32)
    t4 = [self.alloc([16, 16], F32) for _ in range(4)]
    k4 = [self.alloc([1, 16], F32) for _ in range(4)]
    kro = self.alloc([32], BF16)
    ssm = self.alloc([4, 16], F32)
    scs = self.alloc([5, 16], F32)
    cosv = self.alloc([16], F32)
    sinv = self.alloc([16], F32)
    self.sc_ki = self.alloc([16], I32)
    Qtm = self.alloc([16, 96], BF16)
    Ktm = self.alloc([16, 96], BF16)
    w_in = self.wload("mla_win", 0, [8, 672])
    w_qb = self.wload("mla_wqb", 0, [3, 1536])
    w_kvb = self.wload("mla_wkvb", 0, [2, 2048])
    for blk in range(NB):
        tok = slice(blk * 128, (blk + 1) * 128)
        proj, Q32, KV32 = proj_l[blk % 2], Q32_l[blk % 2], KV32_l[blk % 2]
        psA = self.psum(0, [512])
        psB = self.psum(1, [160])
        self.mm_group(psA, [(hT[:, kc, tok].ap, w_in[:, kc, 0:512].ap) for kc in range(8)], reads=[hT, w_in])
        self.mm_group(psB, [(hT[:, kc, tok].ap, w_in[:, kc, 512:672].ap) for kc in range(8)], reads=[hT, w_in])
        self.act(proj[:, 0:512], psA, AF.Copy, scale=rstdc[:, blk:blk + 1])
        self.act(proj[:, 512:672], psB, AF.Copy, scale=rstdc[:, blk:blk + 1])
        self.memset("pool", ss3, 0.0)
        for i, (lo, hi) in enumerate(((0, 384), (384, 640), (640, 672))):
            self.act(self._flat(SQ, hi - lo), proj[:, lo:hi], AF.Square, accum=ss3[:, i:i + 1])
        self.tt("dve", ss3, ss3, self.invn3, ALU.mult)
        self.rsqrt_eps(ss3, ss3)
        self.stt("dve", cqn, proj[:, 0:384], ss3[:, 0:1], rp[:, RP_QA:RP_QA + 384], ALU.mult, ALU.mult)
        self.stt("dve", ckvn, proj[:, 384:640], ss3[:, 1:2], rp[:, RP_KVA:RP_KVA + 256], ALU.mult, ALU.mult)
        self.stt("dve", krn, proj[:, 640:672], ss3[:, 2:3], rp[:, RP_KN + 64:RP_KN + 96], ALU.mult, ALU.mult)
        pt = self.psum(2, [4, 128], BF16)
        for i in range(3):
            self.transpose(pt[:, i, :], cqn[:, i * 128:(i + 1) * 128], self.ident_bf)
        self.copy("act", cqT, pt[:, 0:3, :])
        pt2 = self.psum(3, [4, 128], BF16)
        for i in range(2):
            self.transpose(pt2[:, i, :], ckvn[:, i * 128:(i + 1) * 128], self.ident_bf)
        self.copy("act", ckvT, pt2[:, 0:2, :])
        Qf = self._flat(Q32, 1536)
        for n in range(3):
            ps = self.psum(4 + n % 2, [512])
            self.mm_group(ps, [(cqT[:, kc, :].ap, w_qb[:, kc, n * 512:(n + 1) * 512].ap) for kc in range(3)],
                          reads=[cqT, w_qb])
            self.copy("act", Qf[:, n * 512:(n + 1) * 512], ps)
        KVf = self._flat(KV32, 2048)
        for n in range(4):
            ps = self.psum(6 + n % 2, [512])
            self.mm_group(ps, [(ckvT[:, kc, :].ap, w_kvb[:, kc, n * 512:(n + 1) * 512].ap) for kc in range(2)],
                          reads=[ckvT, w_kvb])
            self.copy("act", KVf[:, n * 512:(n + 1) * 512], ps)
        self.sincos(kb0 + blk, cosv, sinv, scs)
        self.act(SQ, Q32, AF.Square)
        ssn, ssr, ssk = ssm[:, 0, :], ssm[:, 1, :], ssm[:, 2, :]
        self.mk.op("dve", lambda E, o=ssn.ap, i=SQ[:, :, 0:64].ap: E.tensor_reduce(o, i, AX.X, ALU.add),
                   reads=[SQ], writes=[ssn])
        self.mk.op("dve", lambda E, o=ssr.ap, i=SQ[:, :, 64:96].ap: E.tensor_reduce(o, i, AX.X, ALU.add),
                   reads=[SQ], writes=[ssr])
        self.ts("dve", ssn, ssn, 1.0 / 64, None, ALU.mult)
        self.ts("dve", ssr, ssr, 1.0 / 32, None, ALU.mult)
        SQK = SQ[:, :, 0:64]
        self.act(SQK, KV32[:, :, 0:64], AF.Square)
        self.mk.op("dve", lambda E, o=ssk.ap, i=SQK.ap: E.tensor_reduce(o, i, AX.X, ALU.add),
                   reads=[SQK], writes=[ssk])
        self.ts("dve", ssk, ssk, 1.0 / 64, None, ALU.mult)
        r3 = self._flat(ssm, 64)[:, 0:48]
        self.rsqrt_eps(r3, r3)
        self.bt("dve", tn, Q32[:, :, 0:64], ssn, [[1, 16], [0, 64]], ALU.mult)
        self.bt("dve", Qtm[:, :, 0:64], tn, self.qw_row[:, 0:64], [[0, 16], [1, 64]], ALU.mult)
        self.bt("dve", u, Q32[:, :, 64:96], ssr, [[1, 16], [0, 32]], ALU.mult)
        self.bt("dve", u, u, self.qw_row[:, 64:96], [[0, 16], [1, 32]], ALU.mult)
        self.rope("dve", Qtm[:, :, 64:80], Qtm[:, :, 80:96], u[:, :, 0:16], u[:, :, 16:32], cosv, sinv, t4, 16)
        self.bt("dve", Ktm[:, :, 0:64], KV32[:, :, 0:64], ssk, [[1, 16], [0, 64]], ALU.mult)
        kr3 = self._as3(krn, 1, 32)
        ko3 = self._as3(kro, 1, 32)
        self.rope("dve", ko3[:, :, 0:16], ko3[:, :, 16:32], kr3[:, :, 0:16], kr3[:, :, 16:32], cosv, sinv, k4, 1)
        self.mk.op("dve", lambda E, o=Ktm[:, :, 64:96].ap, i=_bc(kro, [[0, 16], [1, 32]]): E.tensor_copy(o, i),
                   reads=[kro], writes=[Ktm[:, :, 64:96]])
        self.copy("act", Vst[:, :, blk, :], KV32[:, :, 64:128])
        for q4 in range(4):
            for (src, dstT, bank) in ((Qtm, QT, 2), (Ktm, KT, 3)):
                ptq = self.psum(bank, [4, 128], BF16)
                for i in range(4):
                    h = q4 * 4 + i
                    self.transpose(ptq[0:96, i, :], src[:, h, :], self.ident_bf)
                self.copy("act", dstT[0:96, q4 * 4:(q4 + 1) * 4, tok], ptq[0:96])
    kdst = self.kc_d[:, :, t0:t0 + T].rearrange("h d t -> d h t")
    self.dma("sp", kdst, KT.ap[0:96], [KT], [Key("kc", t0, t0 + T)], name="kst")
    vdst = self.vc_d[:, :, kb0 * 64:(kb0 + NB) * 64].rearrange("h p x -> p h x")
    self.dma("sp", vdst, Vst.ap.rearrange("p h b d -> p h (b d)"), [Vst], [Key("vc", t0, t0 + T)], name="vst")
    self.mk.label = (self.mk.label or "") + "_attn"
    self.aoff = mark2
    nkb = kb0 + NB
    nkeys = nkb * 128
    KH = [self.alloc([nkeys], BF16), self.alloc([nkeys], BF16)]
    VA = [self.alloc([nkb, 128], BF16), self.alloc([nkb, 128], BF16)]
    PT = [self.alloc([T], BF16) for _ in range(4)]
    OT = self.alloc([8, T], BF16)
    rden = [self.alloc([T], F32), self.alloc([T], F32)]
    self.memset("pool", VA[0][:, :, 64:128], 1.0)
    self.memset("pool", VA[1][:, :, 0:64], 1.0)
    pi = 0
    for h in range(16):
        par = h % 2
        kh = KH[par]
        va = VA[par]
        self.dma("sp", kh.ap[0:96], self.kc_d[h, :, 0:nkeys], [Key("kc", 0, nkeys)], [kh], name="kld")
        vsrc = self.vc_d[h, :, 0:nkb * 64].rearrange("p (kb d) -> p kb d", d=64)
        vdst_v = va[:, :, 0:64] if par == 0 else va[:, :, 64:128]
        self.dma("sp", vdst_v.ap, vsrc, [Key("vc", 0, nkeys)], [vdst_v], name="vld")
        ps_o = self.psum(2 + par, [T])
        ps_d = self.psum(4 + par, [T])
        for kb in range(nkb):
            j = kb - kb0
            q0 = 0 if j <= 0 else j * 128
            N = T - q0
            ps_s = self.psum((0, 1, 4, 5)[kb % 4], [T])
            self.mm_group(ps_s[:, 0:N], [(kh[0:96, kb * 128:(kb + 1) * 128].ap, QT[0:96, h, q0:T].ap)],
                          reads=[kh[:, kb * 128:(kb + 1) * 128], QT[:, h, :]])
            p = PT[pi % 4]
            pi += 1
            self.act(p[:, 0:N], ps_s[:, 0:N], AF.Exp, scale=SCALE)
            if j >= 0:
                self.tt("pool", p[:, 0:128], p[:, 0:128], self.triu_bf, ALU.mult)
            o_ap = ps_o[:, q0:T].ap
            d_ap = ps_d[:, q0:T].ap
            v_ap = va[:, kb, :].ap
            p_ap = p[:, 0:N].ap
            one_ap = self.ones_bf.ap

            def fn(E, o_ap=o_ap, d_ap=d_ap, v_ap=v_ap, p_ap=p_ap, one_ap=one_ap, kb=kb, nkb=nkb):
                return E.matmul(o_ap, v_ap, p_ap, start=(kb == 0), stop=(kb == nkb - 1))
            self.mk.op("pe", fn, reads=[va[:, kb, :], p[:, 0:N]], writes=[ps_o], cost=(N / 2.4 + 8))
        rows = slice(par * 64, par * 64 + 64)
        orows = slice((1 - par) * 64, (1 - par) * 64 + 64)
        rd = rden[par]
        self.mk.op("dve", lambda E, o=rd[orows].ap, i=ps_o[orows].ap: E.reciprocal(o, i), reads=[ps_o], writes=[rd])
        self.tt("dve", OT[rows, h // 2, :], ps_o[rows], rd[orows], ALU.mult)
    for dc in range(8):
        w = self.wload("mla_wout", dc * 128, [8, 128])
        ps = self.psum(6 + dc % 2, [T])
        self.mm_group(ps, [(w[:, c, :].ap, OT[:, c, :].ap) for c in range(8)], reads=[w, OT])
        self.stt("dve", X[:, dc, :], ps, 1.0, X[:, dc, :], ALU.mult, ALU.add)
    self.aoff = mark


def _b_flat(self, v, n):
    return self.view(v.base // 4, [n], v.dtype)


def _b_as3(self, v, a, b_):
    return self.view(v.base // 4, [a, b_], v.dtype)


Builder.mla = _b_mla
Builder.bt = _b_bt
Builder.sincos = _b_sincos
Builder.rope = _b_rope
Builder._flat = _b_flat
Builder._as3 = _b_as3
def _prep_shared(inp):
    f = np.float32
    sh = {}
    g = inp["ffn_w_gate"].reshape(4, 8, 128, NF, 128)
    sh["ffn_g"] = np.ascontiguousarray(g.transpose(0, 3, 2, 1, 4)).reshape(4 * NF * 128, 1024)
    u = inp["ffn_w_up"].reshape(4, 8, 128, NF, 128)
    sh["ffn_u"] = np.ascontiguousarray(u.transpose(0, 3, 2, 1, 4)).reshape(4 * NF * 128, 1024)
    d = inp["ffn_w_down"].reshape(4, NF, 128, 2, 512)
    sh["ffn_d"] = np.ascontiguousarray(d.transpose(0, 1, 3, 2, 4)).reshape(4 * NF * 2 * 128, 512)
    win = inp["ssm_w_in"][0]
    z = win[:, :2048].reshape(8, 128, 4, 512)
    sh["ssm_win_z"] = np.ascontiguousarray(z.transpose(2, 1, 0, 3)).reshape(4 * 128, 8 * 512)
    xb = win[:, 2048:6144].reshape(8, 128, 32, 128)
    sh["ssm_win_x"] = np.ascontiguousarray(xb.transpose(2, 1, 0, 3)).reshape(32 * 128, 8 * 128)
    dtw = win[:, 6144:6176].reshape(8, 128, 32)
    sh["ssm_win_dt"] = np.ascontiguousarray(dtw.transpose(1, 0, 2)).reshape(128, 8 * 32)
    wo = inp["ssm_w_out"][0].reshape(16, 128, 8, 128)
    sh["ssm_wout"] = np.ascontiguousarray(wo.transpose(2, 1, 0, 3)).reshape(8 * 128, 16 * 128)
    mi = inp["mla_w_in"][0].reshape(8, 128, 672)
    sh["mla_win"] = np.ascontiguousarray(mi.transpose(1, 0, 2)).reshape(128, 8 * 672)
    qb = inp["mla_w_q_b"][0].reshape(3, 128, 1536)
    sh["mla_wqb"] = np.ascontiguousarray(qb.transpose(1, 0, 2)).reshape(128, 3 * 1536)
    kvb = inp["mla_w_kv_b"][0].reshape(2, 128, 2048)
    sh["mla_wkvb"] = np.ascontiguousarray(kvb.transpose(1, 0, 2)).reshape(128, 2 * 2048)
    mo = inp["mla_w_out"][0].reshape(8, 128, 8, 128)
    sh["mla_wout"] = np.ascontiguousarray(mo.transpose(2, 1, 0, 3)).reshape(8 * 128, 8 * 128)
    cp = np.zeros((128, CP_N), f)
    cp[:, CP_NORM:CP_NORM + 48] = inp["norm_w"].reshape(6, 8, 128).transpose(2, 0, 1).reshape(128, 48)
    cp[:, CP_CONVW:CP_CONVW + 128] = inp["ssm_conv_w"][0].reshape(4, 32, 128).transpose(2, 1, 0).reshape(128, 128)
    cp[:, CP_CONVB:CP_CONVB + 32] = inp["ssm_conv_b"][0].reshape(32, 128).T
    sh["colpack"] = cp
    rp = np.zeros((128, RP_N), f)

    def rep(off, v):
        rp[:, off:off + v.size] = np.broadcast_to(v.reshape(1, -1), (128, v.size))
    rep(RP_DTB, inp["ssm_dt_bias"][0])
    rep(RP_ALOG, inp["ssm_a_log"][0])
    rep(RP_D, inp["ssm_d"][0])
    rep(RP_SSMNW, inp["ssm_norm_w"][0])
    rep(RP_QA, inp["mla_q_a_norm"][0])
    rep(RP_KVA, inp["mla_kv_a_norm"][0])
    rep(RP_QN, inp["mla_q_norm"][0])
    rep(RP_KN, inp["mla_k_norm"][0])
    rep(RP_INVF, (1.0 / (10000.0 ** (np.arange(0, 32, 2, dtype=f) / f(32)))).astype(f))
    sh["rowpack"] = rp
    ck = np.zeros((128, CK_N), f)
    ck[:, CK_ID:CK_ID + 128] = np.eye(128, dtype=f)
    k = np.arange(128)
    ck[:, CK_TRIU:CK_TRIU + 128] = (k[:, None] <= k[None, :]).astype(f)
    ck[:, CK_USTR:CK_USTR + 128] = (k[:, None] > k[None, :]).astype(f)
    ck[:96, CK_SEL:CK_SEL + 32] = (k[:96, None] % 32 == np.arange(32)[None, :]).astype(f)
    sh["constpack"] = ck
    return sh


_NC_CACHE = {}


def run_cores(inp, n_cores, n_seq, seq_len, stop=None, trace=False):
    key = (n_seq, seq_len, stop)
    sh = _prep_shared(inp)
    x = inp["x"]
    pos = inp["positions"]
    in_maps = []
    for c in range(n_cores):
        m = dict(sh)
        xs = x[c * n_seq:(c + 1) * n_seq, :seq_len, :]
        m["x_fm"] = np.ascontiguousarray(xs.transpose(0, 2, 1))
        ps = pos[c * n_seq:(c + 1) * n_seq, :seq_len].astype(np.int32)
        m["pos_l"] = np.ascontiguousarray(ps.reshape(n_seq, seq_len // 128, 128).transpose(0, 2, 1))
        in_maps.append(m)
    bld = Builder(n_seq=n_seq, seq_len=seq_len, stop=stop)
    bld.mk.use_scopes = trace
    nc = bld.main()
    res = run_bass_kernel_spmd(nc, in_maps, core_ids=list(range(n_cores)), trace=trace)
    outs = [np.ascontiguousarray(r["out_fm"].transpose(0, 2, 1)) for r in res.results]
    return np.concatenate(outs, axis=0), res


def kernel(**inputs):
    inp = {k: np.asarray(v) for k, v in inputs.items()}
    out, _ = run_cores(inp, 8, 2, 4096)
    return out.astype(np.float32)
```
